# Optimizing a Trainium2 kernel written in Bass

```python
import jax, jax.numpy as jnp
from jax import lax
import numpy as np

D_MODEL = 1024
BATCH = 8
SEQ = 4096
DEPTH = 2

N_META = 16
RET_HEADS = 4
RET_HEAD_DIM = 128
RET_WIDTH = RET_HEADS * RET_HEAD_DIM
POOL_WINDOWS = (2, 4, 8, 16)
POOL_GROUPS = len(POOL_WINDOWS)
POOL_GROUP_DIM = 128
POOL_WIDTH = POOL_GROUPS * POOL_GROUP_DIM
CHUNK = 128
D_FF = 2816
ROPE_BASE = 10000.0
EPS = 1e-6
N_IN = 4 * RET_WIDTH + POOL_WIDTH + 2 * D_MODEL

kernel_name = 'hybrid_retention_pool_macaron'


def _rmsnorm(x, g):
    xf = x.astype(jnp.float32)
    y = xf * lax.rsqrt(jnp.mean(xf * xf, axis=-1, keepdims=True) + EPS)
    return (y * g.astype(jnp.float32)).astype(x.dtype)


def _swiglu(h, w_gate, w_up, w_down):
    return (jax.nn.silu(h @ w_gate) * (h @ w_up)) @ w_down


def _rotary(t, pos):
    half = t.shape[-1] // 2
    inv_freq = ROPE_BASE ** (-jnp.arange(half, dtype=jnp.float32) / half)
    ang = pos[:, None] * inv_freq[None, :]
    cos = jnp.cos(ang).astype(t.dtype)
    sin = jnp.sin(ang).astype(t.dtype)
    t1, t2 = t[..., :half], t[..., half:]
    return jnp.concatenate([t1 * cos - t2 * sin, t1 * sin + t2 * cos], axis=-1)


def _retention(q, k, v):
    b, L, _ = q.shape
    pad = (-L) % CHUNK
    Lp = L + pad
    n = Lp // CHUNK

    def heads(t):
        t = t.reshape(b, L, RET_HEADS, RET_HEAD_DIM).transpose(0, 2, 1, 3)
        return jnp.pad(t, ((0, 0), (0, 0), (pad, 0), (0, 0)))

    q, k, v = heads(q), heads(k), heads(v)
    pos = jnp.arange(Lp, dtype=jnp.float32) - pad
    q = _rotary(q, pos) * (RET_HEAD_DIM ** -0.5)
    k = _rotary(k, pos)
    shp = (b, RET_HEADS, n, CHUNK, RET_HEAD_DIM)
    q, k, v = q.reshape(shp), k.reshape(shp), v.reshape(shp)

    dt = q.dtype
    log_gamma = jnp.log1p(-(2.0 ** (-5.0 - jnp.arange(RET_HEADS, dtype=jnp.float32))))
    idx = jnp.arange(CHUNK, dtype=jnp.float32)
    diff = idx[:, None] - idx[None, :]
    intra = jnp.where(diff[None] >= 0, jnp.exp(diff[None] * log_gamma[:, None, None]), 0.0)
    k_decay = jnp.exp((CHUNK - 1.0 - idx)[None, :] * log_gamma[:, None])
    q_decay = jnp.exp((idx + 1.0)[None, :] * log_gamma[:, None])
    chunk_decay = jnp.exp(CHUNK * log_gamma)

    scores = jnp.einsum('bhncd,bhnmd->bhncm', q, k) * intra[None, :, None].astype(dt)
    inner = jnp.einsum('bhncm,bhnmd->bhncd', scores, v)
    kv = jnp.einsum('bhncd,bhnce->nbhde', k * k_decay[None, :, None, :, None].astype(dt), v)
    cd = chunk_decay[None, :, None, None].astype(dt)

    def step(state, kv_n):
        return state * cd + kv_n, state

    s0 = jnp.zeros((b, RET_HEADS, RET_HEAD_DIM, RET_HEAD_DIM), dt)
    _, s_prev = lax.scan(step, s0, kv)
    cross = jnp.einsum('bhncd,nbhde->bhnce', q * q_decay[None, :, None, :, None].astype(dt), s_prev)
    out = (inner + cross).reshape(b, RET_HEADS, Lp, RET_HEAD_DIM)[:, :, pad:]

    of = out.astype(jnp.float32)
    mu = jnp.mean(of, axis=-1, keepdims=True)
    var = jnp.mean(jnp.square(of - mu), axis=-1, keepdims=True)
    of = (of - mu) * lax.rsqrt(var + EPS)
    return of.astype(dt).transpose(0, 2, 1, 3).reshape(b, L, RET_WIDTH)


def _pool_mixer(u, maps, scale):
    b, L, _ = u.shape
    uf = u.astype(jnp.float32)
    t = jnp.arange(L, dtype=jnp.float32)[None, :, None]
    outs = []
    for g, w in enumerate(POOL_WINDOWS):
        ug = uf[..., g * POOL_GROUP_DIM:(g + 1) * POOL_GROUP_DIM]
        cs = jnp.cumsum(ug, axis=1)
        lag = jnp.pad(cs, ((0, 0), (w, 0), (0, 0)))[:, :L]
        mean = (cs - lag) / jnp.minimum(t + 1.0, float(w))
        pooled = (mean - ug).astype(u.dtype)
        outs.append(pooled @ maps[g])
    return jnp.concatenate(outs, axis=-1) * scale


def setup_inputs(seed: int = 0) -> dict:
    key = jax.random.key(seed)
    ks = jax.random.split(key, 20)
    f32 = jnp.float32

    def w(k, shape, fan_in):
        return jax.random.normal(k, shape, f32) * (fan_in ** -0.5)

    def gain(k, shape):
        return 1.0 + 0.02 * jax.random.normal(k, shape, f32)

    return {
        'x': jax.random.normal(ks[0], (BATCH, SEQ, D_MODEL), f32),
        'meta': jax.random.normal(ks[1], (N_META, D_MODEL), f32),
        'ffn1_norm': gain(ks[2], (DEPTH, D_MODEL)),
        'ffn1_gate': w(ks[3], (DEPTH, D_MODEL, D_FF), D_MODEL),
        'ffn1_up': w(ks[4], (DEPTH, D_MODEL, D_FF), D_MODEL),
        'ffn1_down': w(ks[5], (DEPTH, D_FF, D_MODEL), D_FF),
        'mix_norm': gain(ks[6], (DEPTH, D_MODEL)),
        'w_in': w(ks[7], (DEPTH, D_MODEL, N_IN), D_MODEL),
        'pool_maps': w(ks[8], (DEPTH, POOL_GROUPS, POOL_GROUP_DIM, POOL_GROUP_DIM), POOL_GROUP_DIM),
        'pool_scale': gain(ks[9], (DEPTH, POOL_WIDTH)),
        'w_ret_up': w(ks[10], (DEPTH, RET_WIDTH, D_MODEL), RET_WIDTH),
        'w_pool_up': w(ks[11], (DEPTH, POOL_WIDTH, D_MODEL), POOL_WIDTH),
        'w_out': w(ks[12], (DEPTH, D_MODEL, D_MODEL), D_MODEL),
        'ffn2_norm': gain(ks[13], (DEPTH, D_MODEL)),
        'ffn2_gate': w(ks[14], (DEPTH, D_MODEL, D_FF), D_MODEL),
        'ffn2_up': w(ks[15], (DEPTH, D_MODEL, D_FF), D_MODEL),
        'ffn2_down': w(ks[16], (DEPTH, D_FF, D_MODEL), D_FF),
        'final_norm': gain(ks[17], (D_MODEL,)),
    }


def reference(x, meta, ffn1_norm, ffn1_gate, ffn1_up, ffn1_down, mix_norm, w_in,
              pool_maps, pool_scale, w_ret_up, w_pool_up, w_out,
              ffn2_norm, ffn2_gate, ffn2_up, ffn2_down, final_norm):
    b = x.shape[0]
    meta_b = jnp.broadcast_to(meta[None].astype(x.dtype), (b, N_META, D_MODEL))
    h = jnp.concatenate([meta_b, x], axis=1)
    splits = np.cumsum([RET_WIDTH, RET_WIDTH, RET_WIDTH, RET_WIDTH, POOL_WIDTH, D_MODEL])
    for i in range(DEPTH):
        h = h + 0.5 * _swiglu(_rmsnorm(h, ffn1_norm[i]), ffn1_gate[i], ffn1_up[i], ffn1_down[i])
        z = _rmsnorm(h, mix_norm[i]) @ w_in[i]
        q, k, v, g_ret, u_pool, gate_a, gate_b = jnp.split(z, splits, axis=-1)
        ret = (_retention(q, k, v) * jax.nn.silu(g_ret)) @ w_ret_up[i]
        pool = _pool_mixer(u_pool, pool_maps[i], pool_scale[i]) @ w_pool_up[i]
        mixed = jax.nn.sigmoid(gate_a) * ret + jax.nn.sigmoid(gate_b) * pool
        h = h + mixed @ w_out[i]
        h = h + 0.5 * _swiglu(_rmsnorm(h, ffn2_norm[i]), ffn2_gate[i], ffn2_up[i], ffn2_down[i])
    h = _rmsnorm(h, final_norm)
    return h[:, N_META:]
```

```python
import numpy as np
import concourse.bass as bass
import concourse.mybir as mybir
from concourse.bass_utils import run_bass_kernel_spmd

F32 = mybir.dt.float32
BF16 = mybir.dt.bfloat16
AF = mybir.ActivationFunctionType
ALU = mybir.AluOpType
AX = mybir.AxisListType

D = 1024
DFF = 2816
NFF = DFF // 128
NIN = 4608
SEQ = 4096
NMETA = 16
DEPTH = 2
EPS = 1e-6
NSLOT = 6
SLOTW = 4096

WNAMES = ["ffn1_gate", "ffn1_up", "ffn1_down", "w_in", "pool_maps", "w_ret_up",
          "w_pool_up", "w_out", "ffn2_gate", "ffn2_up", "ffn2_down"]
WSHAPES = {"ffn1_gate": [D, DFF], "ffn1_up": [D, DFF], "ffn1_down": [DFF, D],
           "w_in": [D, NIN], "pool_maps": [512, 128], "w_ret_up": [512, D],
           "w_pool_up": [512, D], "w_out": [D, D],
           "ffn2_gate": [D, DFF], "ffn2_up": [D, DFF], "ffn2_down": [DFF, D]}
WGROUP = {"ffn1_gate": 0, "ffn1_up": 0, "ffn1_down": 0, "w_in": 1, "pool_maps": 1,
          "w_ret_up": 1, "w_pool_up": 1, "w_out": 1, "ffn2_gate": 2, "ffn2_up": 2, "ffn2_down": 2}

C_ID = 0
C_MASK = 128
C_CD = 256
C_GAIN = 768
C_PSC = C_GAIN + 56
C_R = C_PSC + 8
NCST = C_R + 64


class Res:
    __slots__ = ("w", "r")

    def __init__(self):
        self.w = None
        self.r = {}


class Sched:
    ENGS = ("pe", "act", "dve", "pool", "sp")

    def __init__(self, sems):
        self.sem = sems
        self.cnt = {e: 0 for e in self.ENGS}
        self.waited = {e: {} for e in self.ENGS}
        self.streams = {e: [] for e in self.ENGS}
        self.dcnt = {}

    def _deps(self, reads, writes):
        deps = {}

        def add(d):
            if d is None:
                return
            k, sem, v = d
            if k not in deps or deps[k][1] < v:
                deps[k] = (sem, v)

        for r in reads:
            add(r.w)
        for w in writes:
            add(w.w)
            for d in w.r.values():
                add(d)
        return deps

    def _wait(self, eng, deps):
        for k, (sem, v) in deps.items():
            if k == "pe" and eng == "pe":
                continue
            if self.waited[eng].get(k, 0) < v:
                self.streams[eng].append(("wait", sem, v))
                self.waited[eng][k] = v

    def op(self, eng, fn, reads=(), writes=()):
        self._wait(eng, self._deps(reads, writes))
        self.cnt[eng] += 1
        seq = self.cnt[eng]
        self.streams[eng].append(("op", fn, self.sem[eng], 1))
        tok = (eng, self.sem[eng], seq)
        for r in reads:
            r.r[eng] = tok
        for w in writes:
            w.w = tok
            w.r = {}

    def group(self, eng, fns, reads=(), writes=()):
        self._wait(eng, self._deps(reads, writes))
        self.cnt[eng] += 1
        seq = self.cnt[eng]
        for f in fns[:-1]:
            self.streams[eng].append(("op", f, None, 0))
        self.streams[eng].append(("op", fns[-1], self.sem[eng], 1))
        tok = (eng, self.sem[eng], seq)
        for r in reads:
            r.r[eng] = tok
        for w in writes:
            w.w = tok
            w.r = {}

    def dma(self, eng, key, sem, fn, reads=(), writes=()):
        self._wait(eng, self._deps(reads, writes))
        self.dcnt[key] = self.dcnt.get(key, 0) + 16
        v = self.dcnt[key]
        self.streams[eng].append(("op", fn, sem, 16))
        tok = (key, sem, v)
        for r in reads:
            r.r[key] = tok
        for w in writes:
            w.w = tok
            w.r = {}
        return tok

    def wait_tok(self, eng, tok):
        k, sem, v = tok
        if self.waited[eng].get(k, 0) < v:
            self.streams[eng].append(("wait", sem, v))
            self.waited[eng][k] = v

    def replay(self, eng, h):
        for it in self.streams[eng]:
            if it[0] == "wait":
                h.wait_ge(it[1], it[2])
            else:
                ins = it[1](h)
                if it[2] is not None:
                    ins.then_inc(it[2], it[3])


def fence(new_res, old_res):
    acc = {}
    for o in old_res:
        for d in ([o.w] if o.w else []) + list(o.r.values()):
            k, sem, v = d
            if k not in acc or acc[k][2] < v:
                acc[k] = d
    for n in new_res:
        n.w = None
        n.r = dict(acc)


def build_program(passes, xrows, layers=DEPTH, do_ffn=True, do_mix=True):
    nc = bass.Bass("TRN2", target_bir_lowering=False)
    TMAX = max(c1 - c0 for c0, c1 in passes) * 128
    nch_total = passes[-1][1]

    x = nc.dram_tensor("x", [xrows, D], F32, kind="ExternalInput").ap()
    meta = nc.dram_tensor("meta", [NMETA, D], F32, kind="ExternalInput").ap()
    cst = nc.dram_tensor("cst", [128, NCST], F32, kind="ExternalInput").ap()
    rot = nc.dram_tensor("rot", [nch_total, 128, 1024], F32, kind="ExternalInput").ap()
    y = nc.dram_tensor("y", [xrows, D], F32, kind="ExternalOutput").ap()
    wf = {}
    wb = {}
    for l in range(DEPTH):
        for n in WNAMES:
            wf[(l, n)] = nc.dram_tensor(f"{n}_{l}", WSHAPES[n], F32, kind="ExternalInput").ap()
            wb[(l, n)] = nc.dram_tensor(f"b_{n}_{l}", WSHAPES[n], BF16, kind="Internal").ap()

    import contextlib
    es = contextlib.ExitStack()
    with es:
        def sb(name, shape, dt):
            return es.enter_context(nc.sbuf_tensor(name, shape, dt))

        def ps(name, shape, dt):
            return es.enter_context(nc.psum_tensor(name, shape, dt))

        def sem(name):
            return es.enter_context(nc.semaphore(name))

        hT = sb("hT", [128, 8, TMAX], F32)
        xnT = sb("xnT", [128, 8, TMAX], BF16)
        arena = sb("arena", [128, NFF * TMAX], BF16)
        slots = [sb(f"slot{i}", [128, SLOTW], BF16) for i in range(NSLOT)]
        cst_sb = sb("cst_sb", [128, NCST], F32)
        ident_bf = sb("ident_bf", [128, 128], BF16)
        ones_bf = sb("ones_bf", [128, 128], BF16)
        maps_sb = [sb(f"maps{l}", [128, 4, 128], BF16) for l in range(DEPTH)]
        S32 = [sb(f"S32_{l}", [128, 4, 128], F32) for l in range(DEPTH)]
        S16 = [sb(f"S16_{l}", [128, 4, 128], BF16) for l in range(DEPTH)]
        halo = [sb(f"halo{l}", [128, 4, 16], F32) for l in range(DEPTH)]
        rot_sb = [sb(f"rot{i}", [128, 1024], F32) for i in range(2)]
        xs = [sb(f"xs{i}", [128, 1024], F32) for i in range(2)]
        sq = sb("sq", [128, 8, 512], BF16)
        stdb = sb("stdb", [128, 512], F32)
        rstd = sb("rstd", [128, 512], F32)
        sgb = [sb(f"sgb{i}", [128, 512], F32) for i in range(2)]
        sgc = [sb(f"sgc{i}", [128, 512], F32) for i in range(2)]
        qtd = [sb(f"qt{i}", [128, 4, 128], BF16) for i in range(2)]
        ktd = [sb(f"kt{i}", [128, 4, 128], BF16) for i in range(2)]
        qf = [sb(f"qf{i}", [128, 4, 128], F32) for i in range(2)]
        kf = [sb(f"kf{i}", [128, 4, 128], F32) for i in range(2)]
        pta = sb("rpa", [128, 4, 64], F32)
        ptb = sb("rpb", [128, 4, 64], F32)
        gst2 = sb("gst2", [128, 4], F32)
        negh = sb("negh", [128, 4], F32)
        rta = sb("rta", [128, 4, 64], F32)
        rtb = sb("rtb", [128, 4, 64], F32)
        vtokd = [sb(f"vtok{i}", [128, 512], BF16) for i in range(2)]
        sgrd = [sb(f"sgr{i}", [128, 512], F32) for i in range(2)]
        qkT = sb("qkT", [128, 8, 128], BF16)
        sT = sb("sT", [128, 4, 128], BF16)
        gt = sb("gt", [128, 4, 128], F32)
        gsq = sb("gsq", [128, 4, 128], F32)
        gy2 = sb("gy2", [128, 4, 128], BF16)
        gst = sb("gst", [128, 16], F32)
        stmp = sb("stmp", [128, 4, 128], F32)
        ub = sb("ub", [128, 528], F32)
        pa = sb("pa", [128, 528], F32)
        pb = sb("pb", [128, 528], F32)
        p16 = sb("p16", [128, 16], F32)
        pooled = sb("pooled", [128, 512], BF16)

        PB = [ps(f"pb{i}", [128, 512], F32) for i in range(7)]
        PT = ps("ptb", [128, 8, 128], BF16)

        esem = {e: sem(f"s_{e}") for e in Sched.ENGS}
        slot_sem = [sem(f"s_slot{i}") for i in range(NSLOT)]
        conv_sem = [sem(f"s_conv{i}") for i in range(3 * DEPTH)]
        misc_sem = sem("s_misc")
        rot_sem = [sem(f"s_rot{i}") for i in range(2)]
        xs_sem = [sem(f"s_xs{i}") for i in range(2)]
        out_sem = [sem(f"s_out{i}") for i in range(2)]

        S = Sched(esem)

        R_hT = {}
        R_xnT = {}

        def rh(j, b):
            return R_hT.setdefault((j, b), Res())

        R_slot = [Res() for _ in range(NSLOT)]
        R_PB = [Res() for _ in range(7)]
        R_PT = Res()
        R_cst = Res()
        R_identbf = Res()
        R_ones = Res()
        R_maps = [Res() for _ in range(DEPTH)]
        R_S32 = [Res() for _ in range(DEPTH)]
        R_S16 = [Res() for _ in range(DEPTH)]
        R_halo = [Res() for _ in range(DEPTH)]
        R_rot = [Res(), Res()]
        R_xs = [Res(), Res()]
        R = {n: Res() for n in ["sq", "stdb", "rstd", "sgb0", "sgb1", "sgc0", "sgc1", "qt", "kt", "rta", "rtb",
                                "vtok", "sgr", "qkT", "sT", "gt", "gsq", "gy2", "gst", "stmp", "ub", "pa",
                                "pb", "p16", "pooled", "pta", "ptb", "gst2"]}
        R_qf = [Res(), Res()]
        R_kf = [Res(), Res()]
        R_qt = [Res(), Res()]
        R_kt = [Res(), Res()]
        R_vt = [Res(), Res()]
        R_sg = [Res(), Res()]
        R_negh = Res()

        conv_tok = {}
        for l in range(DEPTH):
            for n in WNAMES:
                g = l * 3 + WGROUP[n]
                rows = WSHAPES[n][0]
                step = 256
                for r0 in range(0, rows, step):
                    r1 = min(rows, r0 + step)
                    conv_tok[g] = S.dma(
                        "pool", f"conv{g}", conv_sem[g],
                        (lambda src, dst: (lambda h: h.dma_start(out=dst, in_=src, max_dma_last_dim=4096)))(
                            wf[(l, n)][r0:r1, :], wb[(l, n)][r0:r1, :]))

        EPS_AP = sb("eps_ap", [128, 1], F32)
        R_eps = Res()
        S.op("dve", lambda h: h.memset(EPS_AP[:], EPS), writes=[R_eps])
        S.dma("sp", "misc", misc_sem, lambda h: h.dma_start(out=cst_sb[:], in_=cst), writes=[R_cst])
        S.op("dve", lambda h: h.tensor_copy(out=ident_bf[:], in_=cst_sb[:, C_ID:C_ID + 128]),
             reads=[R_cst], writes=[R_identbf])
        S.op("dve", lambda h: h.memset(ones_bf[:], 1.0), writes=[R_ones])
        S.op("dve", lambda h: h.memset(negh[:], -0.5), writes=[R_negh])
        for l in range(DEPTH):
            S.op("dve", (lambda l: lambda h: h.memset(S32[l][:], 0.0))(l), writes=[R_S32[l]])
            S.op("dve", (lambda l: lambda h: h.memset(S16[l][:], 0.0))(l), writes=[R_S16[l]])
            S.op("dve", (lambda l: lambda h: h.memset(halo[l][:], 0.0))(l), writes=[R_halo[l]])
        identf = cst_sb[:, C_ID:C_ID + 128]
        maskT = cst_sb[:, C_MASK:C_MASK + 128]
        CDt = cst_sb[:, C_CD:C_CD + 512].rearrange("p (h e) -> p h e", h=4)

        def gain(n, j):
            return cst_sb[:, C_GAIN + n * 8 + j:C_GAIN + n * 8 + j + 1]

        def pscale(l, g):
            return cst_sb[:, C_PSC + l * 4 + g:C_PSC + l * 4 + g + 1]

        Rtab = cst_sb[:, C_R:C_R + 64].rearrange("p (g t) -> p g t", g=4)

        maps_sem = [sem(f"s_maps{l}") for l in range(DEPTH)]

        ring = {"n": 0}

        def load_tile(src_ap, view_fn, conv_g):
            i = ring["n"] % NSLOT
            ring["n"] += 1
            dst = view_fn(slots[i])
            S.wait_tok("sp", conv_tok[conv_g])
            S.dma("sp", f"slot{i}", slot_sem[i],
                  (lambda d, s: lambda h: h.dma_start(out=d, in_=s))(dst, src_ap), writes=[R_slot[i]])
            return dst, R_slot[i]

        def v_k(ncols):
            return lambda sl: sl[:, 0:8 * ncols].rearrange("p (k n) -> p k n", k=8)

        def v_f(nf, ncols):
            return lambda sl: sl[:, 0:nf * ncols].rearrange("p (f n) -> p f n", f=nf)

        def blocks_of(T):
            out = []
            t = 0
            while t < T:
                nt = min(512, T - t)
                out.append((t, nt))
                t += nt
            return out

        def emit_norm(nidx, blks, final=False):
            for bi, (t0, nt) in enumerate(blks):
                emit_norm_block(nidx, bi, t0, nt, final)

        pending = []

        def pop_pending(n):
            for _ in range(n):
                if pending:
                    pending.pop(0)[1]()

        def flush_pending():
            while pending:
                pending.pop(0)[1]()

        def need_xn(bi):
            if any(b == bi for b, _ in pending):
                flush_pending()

        def norm_p1(bi, t0, nt):
            hres = [rh(j, bi) for j in range(8)]
            S.op("act", lambda h: h.activation(out=sq[:, :, 0:nt], in_=hT[:, :, t0:t0 + nt], func=AF.Square),
                 reads=hres, writes=[R["sq"]])

        def norm_p2(bi, t0, nt):
            fns = []
            for j in range(8):
                fns.append((lambda j: lambda h: h.matmul(
                    PB[6][:, 0:nt], ones_bf[:], sq[:, j, 0:nt], start=(j == 0), stop=(j == 7)))(j))
            S.group("pe", fns, reads=[R["sq"], R_ones], writes=[R_PB[6]])
            S.op("act", lambda h: h.activation(
                out=stdb[:, 0:nt], in_=PB[6][:, 0:nt], func=AF.Sqrt, bias=EPS_AP[:], scale=1.0 / D),
                reads=[R_PB[6], R_eps], writes=[R["stdb"]])
            S.op("dve", lambda h: h.reciprocal(out=rstd[:, 0:nt], in_=stdb[:, 0:nt]),
                 reads=[R["stdb"]], writes=[R["rstd"]])

        def norm_stt(nidx, j, bi, t0, nt, final):
            if final:
                S.op("dve", lambda h: h.scalar_tensor_tensor(
                    out=hT[:, j, t0:t0 + nt], in0=hT[:, j, t0:t0 + nt], scalar=gain(nidx, j),
                    in1=rstd[:, 0:nt], op0=ALU.mult, op1=ALU.mult),
                    reads=[R["rstd"], R_cst, rh(j, bi)], writes=[rh(j, bi)])
            else:
                S.op("dve", lambda h: h.scalar_tensor_tensor(
                    out=xnT[:, j, t0:t0 + nt], in0=hT[:, j, t0:t0 + nt], scalar=gain(nidx, j),
                    in1=rstd[:, 0:nt], op0=ALU.mult, op1=ALU.mult),
                    reads=[R["rstd"], R_cst, rh(j, bi)], writes=[R_xnT.setdefault(bi, Res())])

        def emit_norm_block(nidx, bi, t0, nt, final=False):
            norm_p1(bi, t0, nt)
            norm_p2(bi, t0, nt)
            for j in range(8):
                norm_stt(nidx, j, bi, t0, nt, final)

        def norm_cb(nidx, final):
            def cb(bi, t0, nt):
                norm_p1(bi, t0, nt)
                pending.append((bi, lambda: norm_p2(bi, t0, nt)))
                for j in range(8):
                    pending.append((bi, (lambda j: lambda: norm_stt(nidx, j, bi, t0, nt, final))(j)))
            return cb

        def emit_ffn(l, which, blks, T, old_arena, prenormed=False, after_block=None):
            cg = l * 3 + (0 if which == 1 else 2)
            Wg = wb[(l, f"ffn{which}_gate")].rearrange("(k p) n -> p k n", p=128)
            Wu = wb[(l, f"ffn{which}_up")].rearrange("(k p) n -> p k n", p=128)
            Wd = wb[(l, f"ffn{which}_down")].rearrange("(f p) n -> p f n", p=128)
            hid = arena[:, 0:NFF * T].rearrange("p (f t) -> p f t", f=NFF)
            R_hid = {}
            for f in range(NFF):
                for bi in range(len(blks)):
                    R_hid[(f, bi)] = Res()
            fence(list(R_hid.values()), old_arena)
            if not prenormed:
                emit_norm(l * 3 + (0 if which == 1 else 2), blks)
            pbi = 0
            f0 = 0
            while f0 < NFF:
                nf = min(4, NFF - f0)
                nco = nf * 128
                wg, rg = load_tile(Wg[:, :, f0 * 128:f0 * 128 + nco], v_k(nco), cg)
                wu, ru = load_tile(Wu[:, :, f0 * 128:f0 * 128 + nco], v_k(nco), cg)
                for bi, (t0, nt) in enumerate(blks):
                    need_xn(bi)
                    for fi in range(nf):
                        pop_pending(3)
                        f = f0 + fi
                        pg = pbi % 2
                        pu = 2 + pbi % 2
                        pbi += 1
                        for (w_, r_, pp) in ((wg, rg, pg), (wu, ru, pu)):
                            fns = [(lambda w_, k, fi, t0, nt, pp: lambda h: h.matmul(
                                PB[pp][:, 0:nt], w_[:, k, fi * 128:(fi + 1) * 128], xnT[:, k, t0:t0 + nt],
                                start=(k == 0), stop=(k == 7)))(w_, k, fi, t0, nt, pp) for k in range(8)]
                            S.group("pe", fns, reads=[r_, R_xnT[bi]], writes=[R_PB[pp]])
                        sgi = pg
                        S.op("act", (lambda pg, nt, sgi: lambda h: h.activation(
                            out=sgb[sgi][:, 0:nt], in_=PB[pg][:, 0:nt], func=AF.Silu))(pg, nt, sgi),
                            reads=[R_PB[pg]], writes=[R[f"sgb{sgi}"]])
                        S.op("dve", (lambda f, t0, nt, sgi, pu: lambda h: h.tensor_tensor(
                            out=hid[:, f, t0:t0 + nt], in0=sgb[sgi][:, 0:nt], in1=PB[pu][:, 0:nt],
                            op=ALU.mult))(f, t0, nt, sgi, pu),
                            reads=[R[f"sgb{sgi}"], R_PB[pu]], writes=[R_hid[(f, bi)]])
                f0 += nf
            for bi, (t0, nt) in enumerate(blks):
                for j in range(8):
                    wd, rd = load_tile(Wd[:, :, j * 128:(j + 1) * 128], v_f(NFF, 128), cg)
                    po = 4 + (bi * 8 + j) % 2
                    fns = [(lambda f, t0, nt, po, wd: lambda h: h.matmul(
                        PB[po][:, 0:nt], wd[:, f, :], hid[:, f, t0:t0 + nt],
                        start=(f == 0), stop=(f == NFF - 1)))(f, t0, nt, po, wd) for f in range(NFF)]
                    S.group("pe", fns, reads=[rd] + [R_hid[(f, bi)] for f in range(NFF)], writes=[R_PB[po]])
                    S.op("dve", (lambda j, t0, nt, po: lambda h: h.scalar_tensor_tensor(
                        out=hT[:, j, t0:t0 + nt], in0=PB[po][:, 0:nt], scalar=0.5, in1=hT[:, j, t0:t0 + nt],
                        op0=ALU.mult, op1=ALU.add))(j, t0, nt, po),
                        reads=[R_PB[po], rh(j, bi)], writes=[rh(j, bi)])
                    if j >= 1:
                        pop_pending(2)
                if after_block is not None:
                    after_block(bi, t0, nt)
            return list(R_hid.values())

        def emit_mixer(l, blks, T, c0, nck, old_arena, prenormed=False, after_block=None):
            cg = l * 3 + 1
            Win = wb[(l, "w_in")].rearrange("(k p) n -> p k n", p=128)
            Wru = wb[(l, "w_ret_up")].rearrange("(k p) n -> p k n", p=128)
            Wpu = wb[(l, "w_pool_up")].rearrange("(k p) n -> p k n", p=128)
            Wo = wb[(l, "w_out")].rearrange("(k p) n -> p k n", p=128)
            retinT = arena[:, 0:4 * T].rearrange("p (h t) -> p h t", h=4)
            pmT = arena[:, 4 * T:8 * T].rearrange("p (g t) -> p g t", g=4)
            mixT = arena[:, 8 * T:16 * T].rearrange("p (j t) -> p j t", j=8)
            R_ret = [Res() for _ in blks]
            R_pm = [Res() for _ in blks]
            R_mix = {}
            for j in range(8):
                for bi in range(len(blks)):
                    R_mix[(j, bi)] = Res()
            fence(R_ret + R_pm + list(R_mix.values()), old_arena)

            if c0 == 0:
                S.wait_tok("sp", conv_tok[cg])
                S.dma("sp", f"maps{l}", maps_sem[l],
                      lambda h: h.dma_start(
                          out=maps_sb[l][:], in_=wb[(l, "pool_maps")].rearrange("(g p) n -> p g n", p=128)),
                      writes=[R_maps[l]])
            flush_pending()
            if not prenormed:
                emit_norm(l * 3 + 1, blks)
            wt = [load_tile(Win[:, :, i * 512:(i + 1) * 512], v_k(512), cg) for i in range(4)]
            def m1_A_pe(cl):
                c = c0 + cl
                tcol = cl * 128
                bi = tcol // 512
                ri = c % 2
                S.dma("pool", f"rot{ri}", rot_sem[ri],
                      (lambda ri, c: lambda h: h.dma_start(out=rot_sb[ri][:], in_=rot[c]))(ri, c),
                      writes=[R_rot[ri]])
                for i in range(4):
                    w_, r_ = wt[i]
                    fns = [(lambda w_, k, i, tcol: lambda h: h.matmul(
                        PB[i][:, :], xnT[:, k, tcol:tcol + 128], w_[:, k, :],
                        start=(k == 0), stop=(k == 7)))(w_, k, i, tcol) for k in range(8)]
                    S.group("pe", fns, reads=[r_, R_xnT[bi]], writes=[R_PB[i]])

            def rotary(eng, srcf, rsrc, cs, sn, dst, rdst, ta, rta_, tb, rtb_, ri):
                x1 = srcf[:, :, 0:64]
                x2 = srcf[:, :, 64:128]
                S.op(eng, (lambda x1, cs, ta: lambda h: h.tensor_tensor(out=ta[:], in0=x1, in1=cs, op=ALU.mult))(x1, cs, ta),
                     reads=[rsrc, R_rot[ri]], writes=[rta_])
                S.op(eng, (lambda x2, sn, tb: lambda h: h.tensor_tensor(out=tb[:], in0=x2, in1=sn, op=ALU.mult))(x2, sn, tb),
                     reads=[rsrc, R_rot[ri]], writes=[rtb_])
                S.op(eng, (lambda dst, ta, tb: lambda h: h.tensor_tensor(out=dst[:, :, 0:64], in0=ta[:], in1=tb[:], op=ALU.subtract))(dst, ta, tb),
                     reads=[rta_, rtb_], writes=[rdst])
                S.op(eng, (lambda x1, sn, ta: lambda h: h.tensor_tensor(out=ta[:], in0=x1, in1=sn, op=ALU.mult))(x1, sn, ta),
                     reads=[rsrc, R_rot[ri]], writes=[rta_])
                S.op(eng, (lambda x2, cs, tb: lambda h: h.tensor_tensor(out=tb[:], in0=x2, in1=cs, op=ALU.mult))(x2, cs, tb),
                     reads=[rsrc, R_rot[ri]], writes=[rtb_])
                S.op(eng, (lambda dst, ta, tb: lambda h: h.tensor_tensor(out=dst[:, :, 64:128], in0=ta[:], in1=tb[:], op=ALU.add))(dst, ta, tb),
                     reads=[rta_, rtb_], writes=[rdst])

            def m1_A_rest(cl):
                c = c0 + cl
                par = cl % 2
                ri = c % 2
                S.op("act", (lambda par: lambda h: h.copy(out=qf[par][:], in_=PB[0][:, :].rearrange("p (h e) -> p h e", h=4)))(par),
                     reads=[R_PB[0]], writes=[R_qf[par]])
                S.op("act", (lambda par: lambda h: h.copy(out=kf[par][:], in_=PB[1][:, :].rearrange("p (h e) -> p h e", h=4)))(par),
                     reads=[R_PB[1]], writes=[R_kf[par]])
                S.op("act", (lambda par: lambda h: h.copy(out=vtokd[par][:], in_=PB[2][:, :]))(par),
                     reads=[R_PB[2]], writes=[R_vt[par]])
                S.op("act", (lambda par: lambda h: h.activation(out=sgrd[par][:], in_=PB[3][:, :], func=AF.Silu))(par),
                     reads=[R_PB[3]], writes=[R_sg[par]])
                rt = rot_sb[ri][:].rearrange("p (a h e) -> p a h e", a=4, h=4)
                rotary("pool", qf[par], R_qf[par], rt[:, 0], rt[:, 1], qtd[par], R_qt[par], pta, R["pta"], ptb, R["ptb"], ri)
                rotary("dve", kf[par], R_kf[par], rt[:, 2], rt[:, 3], ktd[par], R_kt[par], rta, R["rta"], rtb, R["rtb"], ri)

            def m1_B(cl):
                par = cl % 2
                qt_, kt_, vt_ = qtd[par], ktd[par], vtokd[par]
                fns = []
                for hh in range(4):
                    fns.append((lambda hh, qt_: lambda h: h.transpose(PT[:, hh, :], qt_[:, hh, :], ident_bf[:]))(hh, qt_))
                for hh in range(4):
                    fns.append((lambda hh, kt_: lambda h: h.transpose(PT[:, 4 + hh, :], kt_[:, hh, :], ident_bf[:]))(hh, kt_))
                S.group("pe", fns, reads=[R_qt[par], R_kt[par], R_identbf], writes=[R_PT])
                S.op("act", lambda h: h.copy(out=qkT[:], in_=PT[:]), reads=[R_PT], writes=[R["qkT"]])
                fns = [(lambda hh: lambda h: h.matmul(
                    PB[4][:, hh * 128:(hh + 1) * 128], qkT[:, 4 + hh, :], qkT[:, hh, :], start=True, stop=True))(hh)
                    for hh in range(4)]
                S.group("pe", fns, reads=[R["qkT"]], writes=[R_PB[4]])
                S.op("dve", lambda h: h.tensor_tensor(
                    out=sT[:], in0=PB[4][:, :].rearrange("p (h c) -> p h c", h=4),
                    in1=maskT.unsqueeze(1).broadcast_to([128, 4, 128]), op=ALU.mult),
                    reads=[R_PB[4], R_cst], writes=[R["sT"]])
                fns = []
                for hh in range(4):
                    fns.append((lambda hh, vt_: lambda h: h.matmul(
                        PB[5][:, hh * 128:(hh + 1) * 128], sT[:, hh, :], vt_[:, hh * 128:(hh + 1) * 128],
                        start=True, stop=False))(hh, vt_))
                    fns.append((lambda hh: lambda h: h.matmul(
                        PB[5][:, hh * 128:(hh + 1) * 128], qkT[:, hh, :], S16[l][:, hh, :],
                        start=False, stop=True))(hh))
                S.group("pe", fns, reads=[R["sT"], R_vt[par], R["qkT"], R_S16[l]], writes=[R_PB[5]])
                fns = [(lambda hh, kt_, vt_: lambda h: h.matmul(
                    PB[6][:, hh * 128:(hh + 1) * 128], kt_[:, hh, :], vt_[:, hh * 128:(hh + 1) * 128],
                    start=True, stop=True))(hh, kt_, vt_) for hh in range(4)]
                S.group("pe", fns, reads=[R_kt[par], R_vt[par]], writes=[R_PB[6]])
                S.op("dve", lambda h: h.tensor_tensor(
                    out=stmp[:], in0=PB[6][:, :].rearrange("p (h e) -> p h e", h=4), in1=S32[l][:], op=ALU.add),
                    reads=[R_PB[6], R_S32[l]], writes=[R["stmp"]])
                S.op("dve", lambda h: h.tensor_tensor(out=S32[l][:], in0=stmp[:], in1=CDt, op=ALU.mult),
                     reads=[R["stmp"], R_cst], writes=[R_S32[l]])
                S.op("act", lambda h: h.copy(out=S16[l][:], in_=S32[l][:]),
                     reads=[R_S32[l]], writes=[R_S16[l]])

            def m1_C(cl):
                par = cl % 2
                P5v = PB[5][:, :].rearrange("p (h e) -> p h e", h=4)
                S.op("dve", lambda h: h.tensor_reduce(out=gst[:, 0:4], in_=P5v, axis=AX.X, op=ALU.add),
                     reads=[R_PB[5]], writes=[R["gst"]])
                S.op("dve", lambda h: h.tensor_scalar(out=gst[:, 4:8], in0=gst[:, 0:4], scalar1=-1.0 / 128,
                                                      scalar2=None, op0=ALU.mult),
                     reads=[R["gst"]], writes=[R["gst"]])
                S.op("dve", lambda h: h.tensor_tensor(
                    out=gt[:], in0=P5v, in1=gst[:, 4:8].unsqueeze(2).broadcast_to([128, 4, 128]), op=ALU.add),
                    reads=[R_PB[5], R["gst"]], writes=[R["gt"]])
                S.op("pool", lambda h: h.tensor_tensor(out=gsq[:], in0=gt[:], in1=gt[:], op=ALU.mult),
                     reads=[R["gt"]], writes=[R["gsq"]])
                S.op("dve", lambda h: h.tensor_reduce(out=gst[:, 8:12], in_=gsq[:], axis=AX.X, op=ALU.add),
                     reads=[R["gsq"], R["gst2"]], writes=[R["gst"]])
                S.op("dve", lambda h: h.tensor_scalar(out=gst[:, 12:16], in0=gst[:, 8:12], scalar1=1.0 / 128,
                                                      scalar2=EPS, op0=ALU.mult, op1=ALU.add),
                     reads=[R["gst"], R["gst2"]], writes=[R["gst"]])
                S.op("pool", lambda h: h.tensor_tensor(out=gst2[:, 0:4], in0=gst[:, 12:16], in1=negh[:, 0:4], op=ALU.pow),
                     reads=[R["gst"], R_negh], writes=[R["gst2"]])
                S.op("pool", lambda h: h.tensor_tensor(
                    out=gsq[:], in0=gt[:], in1=gst2[:, 0:4].unsqueeze(2).broadcast_to([128, 4, 128]), op=ALU.mult),
                    reads=[R["gt"], R["gst2"]], writes=[R["gsq"]])
                S.op("pool", (lambda par: lambda h: h.tensor_tensor(
                    out=gy2[:], in0=gsq[:], in1=sgrd[par][:].rearrange("p (h e) -> p h e", h=4), op=ALU.mult))(par),
                    reads=[R["gsq"], R_sg[par]], writes=[R["gy2"]])

            def m1_D(cl):
                tcol = cl * 128
                bi = tcol // 512
                fns = [(lambda hh: lambda h: h.transpose(PT[:, hh, :], gy2[:, hh, :], ident_bf[:]))(hh)
                       for hh in range(4)]
                S.group("pe", fns, reads=[R["gy2"], R_identbf], writes=[R_PT])
                S.op("act", (lambda tcol: lambda h: h.copy(out=retinT[:, :, tcol:tcol + 128], in_=PT[:, 0:4, :]))(tcol),
                     reads=[R_PT], writes=[R_ret[bi]])

            m1_A_pe(0)
            m1_A_rest(0)
            for cl in range(nck):
                m1_B(cl)
                if cl + 1 < nck:
                    m1_A_pe(cl + 1)
                m1_C(cl)
                if cl + 1 < nck:
                    m1_A_rest(cl + 1)
                m1_D(cl)
            wu_, ru_ = load_tile(Win[:, :, 4 * 512:5 * 512], v_k(512), cg)
            pbi = 0
            for bi, (t0, nt) in enumerate(blks):
                E = 16 + nt
                for g in range(4):
                    pp = pbi % 2
                    pm = 2 + pbi % 2
                    pbi += 1
                    fns = [(lambda k, g, t0, nt, pp: lambda h: h.matmul(
                        PB[pp][:, 0:nt], wu_[:, k, g * 128:(g + 1) * 128], xnT[:, k, t0:t0 + nt],
                        start=(k == 0), stop=(k == 7)))(k, g, t0, nt, pp) for k in range(8)]
                    S.group("pe", fns, reads=[ru_, R_xnT[bi]], writes=[R_PB[pp]])
                    S.op("dve", (lambda l, g: lambda h: h.tensor_copy(out=ub[:, 0:16], in_=halo[l][:, g, :]))(l, g),
                         reads=[R_halo[l]], writes=[R["ub"]])
                    S.op("act", (lambda nt, pp: lambda h: h.copy(out=ub[:, 16:16 + nt], in_=PB[pp][:, 0:nt]))(nt, pp),
                         reads=[R_PB[pp]], writes=[R["ub"]])
                    S.op("dve", (lambda l, g, nt: lambda h: h.tensor_copy(out=halo[l][:, g, :], in_=ub[:, nt:nt + 16]))(l, g, nt),
                         reads=[R["ub"]], writes=[R_halo[l]])
                    src, rsrc = ub, "ub"
                    tmps = [(pa, "pa"), (pb, "pb")]
                    for kk in range(1, g + 2):
                        sh = 2 ** (kk - 1)
                        d0 = 2 ** kk - 1
                        dst, rdst = tmps[(kk - 1) % 2]
                        S.op("dve", (lambda src, dst, sh, d0, E: lambda h: h.tensor_tensor(
                            out=dst[:, d0:E], in0=src[:, d0:E], in1=src[:, d0 - sh:E - sh], op=ALU.add))(src, dst, sh, d0, E),
                            reads=[R[rsrc]], writes=[R[rdst]])
                        src, rsrc = dst, rdst
                    w = 2 ** (g + 1)
                    S.op("dve", (lambda src, nt, w: lambda h: h.scalar_tensor_tensor(
                        out=pooled[:, 0:nt], in0=src[:, 16:16 + nt], scalar=1.0 / w, in1=ub[:, 16:16 + nt],
                        op0=ALU.mult, op1=ALU.subtract))(src, nt, w),
                        reads=[R[rsrc], R["ub"]], writes=[R["pooled"]])
                    if c0 == 0 and bi == 0:
                        S.op("dve", (lambda src, g: lambda h: h.tensor_tensor(
                            out=p16[:], in0=src[:, 16 + 112:16 + 128], in1=Rtab[:, g, :], op=ALU.mult))(src, g),
                            reads=[R[rsrc], R_cst], writes=[R["p16"]])
                        S.op("dve", lambda h: h.tensor_tensor(
                            out=pooled[:, 112:128], in0=p16[:], in1=ub[:, 16 + 112:16 + 128], op=ALU.subtract),
                            reads=[R["p16"], R["ub"], R["pooled"]], writes=[R["pooled"]])
                    S.op("pe", (lambda l, g, nt, pm: lambda h: h.matmul(
                        PB[pm][:, 0:nt], maps_sb[l][:, g, :], pooled[:, 0:nt], start=True, stop=True))(l, g, nt, pm),
                        reads=[R_maps[l], R["pooled"]], writes=[R_PB[pm]])
                    S.op("act", (lambda l, g, t0, nt, pm: lambda h: h.activation(
                        out=pmT[:, g, t0:t0 + nt], in_=PB[pm][:, 0:nt], func=AF.Copy, scale=pscale(l, g)))(l, g, t0, nt, pm),
                        reads=[R_PB[pm], R_cst], writes=[R_pm[bi]])
            wru, rru = load_tile(Wru, v_f(4, 1024), cg)
            wpu, rpu = load_tile(Wpu, v_f(4, 1024), cg)
            it = 0
            for jg in range(2):
                wga, rga = load_tile(Win[:, :, (5 + jg) * 512:(6 + jg) * 512], v_k(512), cg)
                wgb, rgb = load_tile(Win[:, :, (7 + jg) * 512:(8 + jg) * 512], v_k(512), cg)
                for jj in range(4):
                    j = jg * 4 + jj
                    for bi, (t0, nt) in enumerate(blks):
                        fns = [(lambda k, jj, t0, nt, wga: lambda h: h.matmul(
                            PB[0][:, 0:nt], wga[:, k, jj * 128:(jj + 1) * 128], xnT[:, k, t0:t0 + nt],
                            start=(k == 0), stop=(k == 7)))(k, jj, t0, nt, wga) for k in range(8)]
                        S.group("pe", fns, reads=[rga, R_xnT[bi]], writes=[R_PB[0]])
                        fns = [(lambda k, jj, t0, nt, wgb: lambda h: h.matmul(
                            PB[1][:, 0:nt], wgb[:, k, jj * 128:(jj + 1) * 128], xnT[:, k, t0:t0 + nt],
                            start=(k == 0), stop=(k == 7)))(k, jj, t0, nt, wgb) for k in range(8)]
                        S.group("pe", fns, reads=[rgb, R_xnT[bi]], writes=[R_PB[1]])
                        fns = [(lambda hh, j, t0, nt: lambda h: h.matmul(
                            PB[2][:, 0:nt], wru[:, hh, j * 128:(j + 1) * 128], retinT[:, hh, t0:t0 + nt],
                            start=(hh == 0), stop=(hh == 3)))(hh, j, t0, nt) for hh in range(4)]
                        S.group("pe", fns, reads=[rru, R_ret[bi]], writes=[R_PB[2]])
                        fns = [(lambda g, j, t0, nt: lambda h: h.matmul(
                            PB[3][:, 0:nt], wpu[:, g, j * 128:(j + 1) * 128], pmT[:, g, t0:t0 + nt],
                            start=(g == 0), stop=(g == 3)))(g, j, t0, nt) for g in range(4)]
                        S.group("pe", fns, reads=[rpu, R_pm[bi]], writes=[R_PB[3]])
                        a = it % 2
                        it += 1
                        S.op("act", (lambda nt, a: lambda h: h.activation(
                            out=sgb[a][:, 0:nt], in_=PB[0][:, 0:nt], func=AF.Sigmoid))(nt, a),
                            reads=[R_PB[0]], writes=[R[f"sgb{a}"]])
                        S.op("act", (lambda nt, a: lambda h: h.activation(
                            out=sgc[a][:, 0:nt], in_=PB[1][:, 0:nt], func=AF.Sigmoid))(nt, a),
                            reads=[R_PB[1]], writes=[R[f"sgc{a}"]])
                        S.op("dve", (lambda nt, a: lambda h: h.tensor_tensor(
                            out=sgb[a][:, 0:nt], in0=sgb[a][:, 0:nt], in1=PB[2][:, 0:nt], op=ALU.mult))(nt, a),
                            reads=[R[f"sgb{a}"], R_PB[2]], writes=[R[f"sgb{a}"]])
                        S.op("dve", (lambda nt, a: lambda h: h.tensor_tensor(
                            out=sgc[a][:, 0:nt], in0=sgc[a][:, 0:nt], in1=PB[3][:, 0:nt], op=ALU.mult))(nt, a),
                            reads=[R[f"sgc{a}"], R_PB[3]], writes=[R[f"sgc{a}"]])
                        S.op("dve", (lambda j, t0, nt, a: lambda h: h.tensor_tensor(
                            out=mixT[:, j, t0:t0 + nt], in0=sgb[a][:, 0:nt], in1=sgc[a][:, 0:nt], op=ALU.add))(j, t0, nt, a),
                            reads=[R[f"sgb{a}"], R[f"sgc{a}"]], writes=[R_mix[(j, bi)]])
            wos = [load_tile(Wo[:, :, jh * 512:(jh + 1) * 512], v_k(512), cg) for jh in range(2)]
            for bi, (t0, nt) in enumerate(blks):
                for jh in range(2):
                    wo, ro = wos[jh]
                    for jj in range(4):
                        j2 = jh * 4 + jj
                        po = 4 + (bi * 8 + j2) % 2
                        fns = [(lambda k, jj, t0, nt, po, wo: lambda h: h.matmul(
                            PB[po][:, 0:nt], wo[:, k, jj * 128:(jj + 1) * 128], mixT[:, k, t0:t0 + nt],
                            start=(k == 0), stop=(k == 7)))(k, jj, t0, nt, po, wo) for k in range(8)]
                        S.group("pe", fns, reads=[ro] + [R_mix[(k, bi)] for k in range(8)], writes=[R_PB[po]])
                        S.op("dve", (lambda j2, t0, nt, po: lambda h: h.tensor_tensor(
                            out=hT[:, j2, t0:t0 + nt], in0=PB[po][:, 0:nt], in1=hT[:, j2, t0:t0 + nt], op=ALU.add))(j2, t0, nt, po),
                            reads=[R_PB[po], rh(j2, bi)], writes=[rh(j2, bi)])
                        if j2 >= 1:
                            pop_pending(2)
                if after_block is not None:
                    after_block(bi, t0, nt)
            return R_ret + R_pm + list(R_mix.values())


        out_n = 0
        old_arena = []
        for (c0, c1) in passes:
            nck = c1 - c0
            T = nck * 128
            blks = blocks_of(T)
            for cl in range(nck):
                c = c0 + cl
                xi = c % 2
                if c == 0:
                    S.op("dve", (lambda xi: lambda h: h.memset(xs[xi][:], 0.0))(xi), writes=[R_xs[xi]])
                    S.dma("sp", f"xs{xi}", xs_sem[xi],
                          (lambda xi: lambda h: h.dma_start(out=xs[xi][112:128, :], in_=meta))(xi),
                          reads=[R_xs[xi]], writes=[R_xs[xi]])
                else:
                    S.dma("sp", f"xs{xi}", xs_sem[xi],
                          (lambda xi, c: lambda h: h.dma_start(out=xs[xi][:], in_=x[(c - 1) * 128:c * 128, :]))(xi, c),
                          writes=[R_xs[xi]])
                tcol = cl * 128
                bi = tcol // 512
                for half in range(2):
                    pbk = half
                    fns = [(lambda jj, half, xi, pbk: lambda h: h.transpose(
                        PB[pbk][:, jj * 128:(jj + 1) * 128], xs[xi][:, (half * 4 + jj) * 128:(half * 4 + jj + 1) * 128],
                        identf))(jj, half, xi, pbk) for jj in range(4)]
                    S.group("pe", fns, reads=[R_xs[xi], R_cst], writes=[R_PB[pbk]])
                    S.op("act", (lambda half, tcol, pbk: lambda h: h.copy(
                        out=hT[:, half * 4:half * 4 + 4, tcol:tcol + 128],
                        in_=PB[pbk][:, :].rearrange("p (j t) -> p j t", j=4)))(half, tcol, pbk),
                        reads=[R_PB[pbk]], writes=[rh(j, bi) for j in range(half * 4, half * 4 + 4)])
            stages = []
            for l in range(layers):
                if do_ffn:
                    stages.append(("ffn", l, 1, l * 3 + 0))
                if do_mix:
                    stages.append(("mix", l, 0, l * 3 + 1))
                if do_ffn:
                    stages.append(("ffn", l, 2, l * 3 + 2))
            for si, (kind, l, which, nidx) in enumerate(stages):
                if si + 1 < len(stages):
                    nxt = stages[si + 1][3]
                    cb = norm_cb(nxt, False)
                else:
                    cb = norm_cb(6, True)
                if kind == "ffn":
                    old_arena = emit_ffn(l, which, blks, T, old_arena, prenormed=(si > 0), after_block=cb)
                else:
                    old_arena = emit_mixer(l, blks, T, c0, nck, old_arena, prenormed=(si > 0), after_block=cb)
            flush_pending()
            for cl in range(nck):
                c = c0 + cl
                if c == 0:
                    continue
                tcol = cl * 128
                bi = tcol // 512
                xi = out_n % 2
                out_n += 1
                for half in range(2):
                    pbk = 2 + half
                    fns = [(lambda jj, half, tcol, pbk: lambda h: h.transpose(
                        PB[pbk][:, jj * 128:(jj + 1) * 128], hT[:, half * 4 + jj, tcol:tcol + 128], identf))(jj, half, tcol, pbk)
                        for jj in range(4)]
                    S.group("pe", fns, reads=[rh(j, bi) for j in range(half * 4, half * 4 + 4)] + [R_cst],
                            writes=[R_PB[pbk]])
                    S.op("act", (lambda half, xi, pbk: lambda h: h.copy(
                        out=xs[xi][:, half * 512:(half + 1) * 512], in_=PB[pbk][:, :]))(half, xi, pbk),
                        reads=[R_PB[pbk]], writes=[R_xs[xi]])
                S.dma("pool", f"xs{xi}", xs_sem[xi],
                      (lambda xi, c: lambda h: h.dma_start(out=y[(c - 1) * 128:c * 128, :], in_=xs[xi][:]))(xi, c),
                      reads=[R_xs[xi]])
        for xi in range(2):
            if R_xs[xi].r:
                for tok in R_xs[xi].r.values():
                    S.wait_tok("pool", tok)

        with nc.Block() as block:
            @block.tensor
            def _(h):
                S.replay("pe", h)

            @block.scalar
            def _(h):
                S.replay("act", h)

            @block.vector
            def _(h):
                S.replay("dve", h)

            @block.gpsimd
            def _(h):
                S.replay("pool", h)

            @block.sync
            def _(h):
                S.replay("sp", h)
    return nc


def make_consts(nch_total, inputs):
    cst = np.zeros((128, NCST), np.float32)
    cst[:, C_ID:C_ID + 128] = np.eye(128, dtype=np.float32)
    m = np.arange(128)[:, None]
    c = np.arange(128)[None, :]
    cst[:, C_MASK:C_MASK + 128] = (c >= m).astype(np.float32)
    gam = 1.0 - 2.0 ** (-5.0 - np.arange(4, dtype=np.float64))
    for h in range(4):
        cst[:, C_CD + h * 128:C_CD + (h + 1) * 128] = np.float32(gam[h] ** 128)
    for w_i, w in enumerate((2, 4, 8, 16)):
        t = np.arange(16)
        cst[:, C_R + w_i * 16:C_R + (w_i + 1) * 16] = (1.0 / np.minimum(t + 1.0, float(w))).astype(np.float32)[None, :]
    half = 64
    inv_freq = (np.float32(10000.0) ** (-np.arange(half, dtype=np.float32) / half)).astype(np.float32)
    rot = np.zeros((nch_total, 128, 4, 4, 64), np.float32)
    i = np.arange(128, dtype=np.float64)
    for ch in range(nch_total):
        pos = (np.arange(128, dtype=np.float32) + np.float32(128 * ch - 112)).astype(np.float32)
        ang = (pos[:, None] * inv_freq[None, :]).astype(np.float32)
        cs = np.cos(ang).astype(np.float64)
        sn = np.sin(ang).astype(np.float64)
        for h in range(4):
            qs = (128.0 ** -0.5) * gam[h] ** (i + 1.0)
            ks = gam[h] ** (-(i + 1.0))
            rot[ch, :, 0, h, :] = cs * qs[:, None]
            rot[ch, :, 1, h, :] = sn * qs[:, None]
            rot[ch, :, 2, h, :] = cs * ks[:, None]
            rot[ch, :, 3, h, :] = sn * ks[:, None]
    return cst, rot.reshape(nch_total, 128, 1024)


def fill_param_consts(cst, inputs):
    norms = [inputs["ffn1_norm"][0], inputs["mix_norm"][0], inputs["ffn2_norm"][0],
             inputs["ffn1_norm"][1], inputs["mix_norm"][1], inputs["ffn2_norm"][1], inputs["final_norm"]]
    for n, g in enumerate(norms):
        cst[:, C_GAIN + n * 8:C_GAIN + (n + 1) * 8] = np.asarray(g, np.float32).reshape(8, 128).T
    for l in range(DEPTH):
        cst[:, C_PSC + l * 4:C_PSC + (l + 1) * 4] = np.asarray(inputs["pool_scale"][l], np.float32).reshape(4, 128).T
    return cst


PASSES_FULL = [(0, 6), (6, 12), (12, 18), (18, 24), (24, 30), (30, 33)]


def run(inputs, passes, xrows, n_cores, **kw):
    inputs = {k: np.asarray(v) for k, v in inputs.items()}
    nch_total = passes[-1][1]
    cst, rot = make_consts(nch_total, inputs)
    cst = fill_param_consts(cst, inputs)
    nc = build_program(passes, xrows, **kw)
    base = {"meta": np.ascontiguousarray(inputs["meta"], np.float32), "cst": cst, "rot": rot}
    for l in range(DEPTH):
        for n in WNAMES:
            a = np.asarray(inputs[n][l], np.float32)
            base[f"{n}_{l}"] = np.ascontiguousarray(a.reshape(WSHAPES[n]))
    in_maps = []
    for b in range(n_cores):
        m = dict(base)
        m["x"] = np.ascontiguousarray(inputs["x"][b, :xrows], np.float32)
        in_maps.append(m)
    res = run_bass_kernel_spmd(nc, in_maps, core_ids=list(range(n_cores)))
    return np.stack([r["y"] for r in res.results], axis=0)


def kernel(**inputs):
    return run(inputs, PASSES_FULL, SEQ, 8).astype(np.float32)
```

```python
import numpy as np
import concourse.bass as bass
import concourse.mybir as mybir
from concourse.bass_utils import run_bass_kernel_spmd

F32 = mybir.dt.float32
BF16 = mybir.dt.bfloat16
AF = mybir.ActivationFunctionType
ALU = mybir.AluOpType
AX = mybir.AxisListType

D = 1024
DFF = 2816
NFF = DFF // 128
NIN = 4608
SEQ = 4096
NMETA = 16
DEPTH = 2
EPS = 1e-6
NSLOT = 6
SLOTW = 4096

WNAMES = ["ffn1_gate", "ffn1_up", "ffn1_down", "w_in", "pool_maps", "w_ret_up",
          "w_pool_up", "w_out", "ffn2_gate", "ffn2_up", "ffn2_down"]
WSHAPES = {"ffn1_gate": [D, DFF], "ffn1_up": [D, DFF], "ffn1_down": [DFF, D],
           "w_in": [D, NIN], "pool_maps": [512, 128], "w_ret_up": [512, D],
           "w_pool_up": [512, D], "w_out": [D, D],
           "ffn2_gate": [D, DFF], "ffn2_up": [D, DFF], "ffn2_down": [DFF, D]}
WGROUP = {"ffn1_gate": 0, "ffn1_up": 0, "ffn1_down": 0, "w_in": 1, "pool_maps": 1,
          "w_ret_up": 1, "w_pool_up": 1, "w_out": 1, "ffn2_gate": 2, "ffn2_up": 2, "ffn2_down": 2}

C_ID = 0
C_MASK = 128
C_CD = 256
C_GAIN = 768
C_PSC = C_GAIN + 56
C_R = C_PSC + 8
NCST = C_R + 64


class Res:
    __slots__ = ("w", "r")

    def __init__(self):
        self.w = None
        self.r = {}


class Sched:
    ENGS = ("pe", "act", "dve", "pool", "sp")

    def __init__(self, sems):
        self.sem = sems
        self.cnt = {e: 0 for e in self.ENGS}
        self.waited = {e: {} for e in self.ENGS}
        self.streams = {e: [] for e in self.ENGS}
        self.dcnt = {}

    def _deps(self, reads, writes):
        deps = {}

        def add(d):
            if d is None:
                return
            k, sem, v = d
            if k not in deps or deps[k][1] < v:
                deps[k] = (sem, v)

        for r in reads:
            add(r.w)
        for w in writes:
            add(w.w)
            for d in w.r.values():
                add(d)
        return deps

    def _wait(self, eng, deps):
        for k, (sem, v) in deps.items():
            if k == "pe" and eng == "pe":
                continue
            if self.waited[eng].get(k, 0) < v:
                self.streams[eng].append(("wait", sem, v))
                self.waited[eng][k] = v

    def op(self, eng, fn, reads=(), writes=()):
        self._wait(eng, self._deps(reads, writes))
        self.cnt[eng] += 1
        seq = self.cnt[eng]
        self.streams[eng].append(("op", fn, self.sem[eng], 1))
        tok = (eng, self.sem[eng], seq)
        for r in reads:
            r.r[eng] = tok
        for w in writes:
            w.w = tok
            w.r = {}

    def group(self, eng, fns, reads=(), writes=()):
        self._wait(eng, self._deps(reads, writes))
        self.cnt[eng] += 1
        seq = self.cnt[eng]
        for f in fns[:-1]:
            self.streams[eng].append(("op", f, None, 0))
        self.streams[eng].append(("op", fns[-1], self.sem[eng], 1))
        tok = (eng, self.sem[eng], seq)
        for r in reads:
            r.r[eng] = tok
        for w in writes:
            w.w = tok
            w.r = {}

    def dma(self, eng, key, sem, fn, reads=(), writes=()):
        self._wait(eng, self._deps(reads, writes))
        self.dcnt[key] = self.dcnt.get(key, 0) + 16
        v = self.dcnt[key]
        self.streams[eng].append(("op", fn, sem, 16))
        tok = (key, sem, v)
        for r in reads:
            r.r[key] = tok
        for w in writes:
            w.w = tok
            w.r = {}
        return tok

    def wait_tok(self, eng, tok):
        k, sem, v = tok
        if self.waited[eng].get(k, 0) < v:
            self.streams[eng].append(("wait", sem, v))
            self.waited[eng][k] = v

    def replay(self, eng, h):
        for it in self.streams[eng]:
            if it[0] == "wait":
                h.wait_ge(it[1], it[2])
            else:
                ins = it[1](h)
                if it[2] is not None:
                    ins.then_inc(it[2], it[3])


def fence(new_res, old_res):
    acc = {}
    for o in old_res:
        for d in ([o.w] if o.w else []) + list(o.r.values()):
            k, sem, v = d
            if k not in acc or acc[k][2] < v:
                acc[k] = d
    for n in new_res:
        n.w = None
        n.r = dict(acc)


def build_program(passes, xrows, layers=DEPTH, do_ffn=True, do_mix=True):
    nc = bass.Bass("TRN2", target_bir_lowering=False)
    TMAX = max(c1 - c0 for c0, c1 in passes) * 128
    nch_total = passes[-1][1]

    x = nc.dram_tensor("x", [xrows, D], F32, kind="ExternalInput").ap()
    meta = nc.dram_tensor("meta", [NMETA, D], F32, kind="ExternalInput").ap()
    cst = nc.dram_tensor("cst", [128, NCST], F32, kind="ExternalInput").ap()
    rot = nc.dram_tensor("rot", [nch_total, 128, 1024], F32, kind="ExternalInput").ap()
    y = nc.dram_tensor("y", [xrows, D], F32, kind="ExternalOutput").ap()
    wf = {}
    wb = {}
    for l in range(DEPTH):
        for n in WNAMES:
            wf[(l, n)] = nc.dram_tensor(f"{n}_{l}", WSHAPES[n], F32, kind="ExternalInput").ap()
            wb[(l, n)] = nc.dram_tensor(f"b_{n}_{l}", WSHAPES[n], BF16, kind="Internal").ap()

    import contextlib
    es = contextlib.ExitStack()
    with es:
        def sb(name, shape, dt):
            return es.enter_context(nc.sbuf_tensor(name, shape, dt))

        def ps(name, shape, dt):
            return es.enter_context(nc.psum_tensor(name, shape, dt))

        def sem(name):
            return es.enter_context(nc.semaphore(name))

        hT = sb("hT", [128, 8, TMAX], F32)
        xnT = sb("xnT", [128, 8, TMAX], BF16)
        arena = sb("arena", [128, NFF * TMAX], BF16)
        slots = [sb(f"slot{i}", [128, SLOTW], BF16) for i in range(NSLOT)]
        cst_sb = sb("cst_sb", [128, NCST], F32)
        ident_bf = sb("ident_bf", [128, 128], BF16)
        ones_bf = sb("ones_bf", [128, 128], BF16)
        maps_sb = [sb(f"maps{l}", [128, 4, 128], BF16) for l in range(DEPTH)]
        S32 = [sb(f"S32_{l}", [128, 4, 128], F32) for l in range(DEPTH)]
        S16 = [sb(f"S16_{l}", [128, 4, 128], BF16) for l in range(DEPTH)]
        halo = [sb(f"halo{l}", [128, 4, 16], F32) for l in range(DEPTH)]
        rot_sb = [sb(f"rot{i}", [128, 1024], F32) for i in range(2)]
        xs = [sb(f"xs{i}", [128, 1024], F32) for i in range(2)]
        sq = sb("sq", [128, 8, 512], BF16)
        stdb = sb("stdb", [128, 512], F32)
        rstd = sb("rstd", [128, 512], F32)
        sgb = [sb(f"sgb{i}", [128, 512], F32) for i in range(2)]
        sgc = [sb(f"sgc{i}", [128, 512], F32) for i in range(2)]
        qtd = [sb(f"qt{i}", [128, 4, 128], BF16) for i in range(2)]
        ktd = [sb(f"kt{i}", [128, 4, 128], BF16) for i in range(2)]
        qf = [sb(f"qf{i}", [128, 4, 128], F32) for i in range(2)]
        kf = [sb(f"kf{i}", [128, 4, 128], F32) for i in range(2)]
        pta = sb("rpa", [128, 4, 64], F32)
        ptb = sb("rpb", [128, 4, 64], F32)
        gst2 = sb("gst2", [128, 4], F32)
        negh = sb("negh", [128, 4], F32)
        rta = sb("rta", [128, 4, 64], F32)
        rtb = sb("rtb", [128, 4, 64], F32)
        vtokd = [sb(f"vtok{i}", [128, 512], BF16) for i in range(2)]
        sgrd = [sb(f"sgr{i}", [128, 512], F32) for i in range(2)]
        qkT = sb("qkT", [128, 8, 128], BF16)
        sT = sb("sT", [128, 4, 128], BF16)
        gt = sb("gt", [128, 4, 128], F32)
        gsq = sb("gsq", [128, 4, 128], F32)
        gy2 = sb("gy2", [128, 4, 128], BF16)
        gst = sb("gst", [128, 16], F32)
        stmp = sb("stmp", [128, 4, 128], F32)
        ub = sb("ub", [128, 528], F32)
        pa = sb("pa", [128, 528], F32)
        pb = sb("pb", [128, 528], F32)
        p16 = sb("p16", [128, 16], F32)
        pooled = sb("pooled", [128, 512], BF16)

        PB = [ps(f"pb{i}", [128, 512], F32) for i in range(7)]
        PT = ps("ptb", [128, 8, 128], BF16)

        esem = {e: sem(f"s_{e}") for e in Sched.ENGS}
        slot_sem = [sem(f"s_slot{i}") for i in range(NSLOT)]
        conv_sem = [sem(f"s_conv{i}") for i in range(3 * DEPTH)]
        misc_sem = sem("s_misc")
        rot_sem = [sem(f"s_rot{i}") for i in range(2)]
        xs_sem = [sem(f"s_xs{i}") for i in range(2)]
        out_sem = [sem(f"s_out{i}") for i in range(2)]

        S = Sched(esem)

        R_hT = {}
        R_xnT = {}

        def rh(j, b):
            return R_hT.setdefault((j, b), Res())

        R_slot = [Res() for _ in range(NSLOT)]
        R_PB = [Res() for _ in range(7)]
        R_PT = Res()
        R_cst = Res()
        R_identbf = Res()
        R_ones = Res()
        R_maps = [Res() for _ in range(DEPTH)]
        R_S32 = [Res() for _ in range(DEPTH)]
        R_S16 = [Res() for _ in range(DEPTH)]
        R_halo = [Res() for _ in range(DEPTH)]
        R_rot = [Res(), Res()]
        R_xs = [Res(), Res()]
        R = {n: Res() for n in ["sq", "stdb", "rstd", "sgb0", "sgb1", "sgc0", "sgc1", "qt", "kt", "rta", "rtb",
                                "vtok", "sgr", "qkT", "sT", "gt", "gsq", "gy2", "gst", "stmp", "ub", "pa",
                                "pb", "p16", "pooled", "pta", "ptb", "gst2"]}
        R_qf = [Res(), Res()]
        R_kf = [Res(), Res()]
        R_qt = [Res(), Res()]
        R_kt = [Res(), Res()]
        R_vt = [Res(), Res()]
        R_sg = [Res(), Res()]
        R_negh = Res()

        conv_tok = {}

        def emit_conv(g, after=()):
            l = g // 3
            first = True
            for n in WNAMES:
                if WGROUP[n] != g % 3:
                    continue
                rows = WSHAPES[n][0]
                step = 256
                for r0 in range(0, rows, step):
                    r1 = min(rows, r0 + step)
                    conv_tok[g] = S.dma(
                        "pool", f"conv{g}", conv_sem[g],
                        (lambda src, dst: (lambda h: h.dma_start(out=dst, in_=src, max_dma_last_dim=4096)))(
                            wf[(l, n)][r0:r1, :], wb[(l, n)][r0:r1, :]),
                        reads=list(after) if first else ())
                    first = False

        staged_conv = False
        for g_ in range(3 * DEPTH if not staged_conv else 2):
            emit_conv(g_)

        EPS_AP = sb("eps_ap", [128, 1], F32)
        R_eps = Res()
        S.op("dve", lambda h: h.memset(EPS_AP[:], EPS), writes=[R_eps])
        S.dma("sp", "misc", misc_sem, lambda h: h.dma_start(out=cst_sb[:], in_=cst), writes=[R_cst])
        S.op("dve", lambda h: h.tensor_copy(out=ident_bf[:], in_=cst_sb[:, C_ID:C_ID + 128]),
             reads=[R_cst], writes=[R_identbf])
        S.op("dve", lambda h: h.memset(ones_bf[:], 1.0), writes=[R_ones])
        S.op("dve", lambda h: h.memset(negh[:], -0.5), writes=[R_negh])
        for l in range(DEPTH):
            S.op("dve", (lambda l: lambda h: h.memset(S32[l][:], 0.0))(l), writes=[R_S32[l]])
            S.op("dve", (lambda l: lambda h: h.memset(S16[l][:], 0.0))(l), writes=[R_S16[l]])
            S.op("dve", (lambda l: lambda h: h.memset(halo[l][:], 0.0))(l), writes=[R_halo[l]])
        identf = cst_sb[:, C_ID:C_ID + 128]
        maskT = cst_sb[:, C_MASK:C_MASK + 128]
        CDt = cst_sb[:, C_CD:C_CD + 512].rearrange("p (h e) -> p h e", h=4)

        def gain(n, j):
            return cst_sb[:, C_GAIN + n * 8 + j:C_GAIN + n * 8 + j + 1]

        def pscale(l, g):
            return cst_sb[:, C_PSC + l * 4 + g:C_PSC + l * 4 + g + 1]

        Rtab = cst_sb[:, C_R:C_R + 64].rearrange("p (g t) -> p g t", g=4)

        maps_sem = [sem(f"s_maps{l}") for l in range(DEPTH)]

        ring = {"n": 0}

        def load_tile(src_ap, view_fn, conv_g):
            i = ring["n"] % NSLOT
            ring["n"] += 1
            dst = view_fn(slots[i])
            S.wait_tok("sp", conv_tok[conv_g])
            S.dma("sp", f"slot{i}", slot_sem[i],
                  (lambda d, s: lambda h: h.dma_start(out=d, in_=s))(dst, src_ap), writes=[R_slot[i]])
            return dst, R_slot[i]

        def v_k(ncols):
            return lambda sl: sl[:, 0:8 * ncols].rearrange("p (k n) -> p k n", k=8)

        def v_f(nf, ncols):
            return lambda sl: sl[:, 0:nf * ncols].rearrange("p (f n) -> p f n", f=nf)

        def blocks_of(T):
            out = []
            t = 0
            while t < T:
                nt = min(512, T - t)
                out.append((t, nt))
                t += nt
            return out

        def emit_norm(nidx, blks, final=False):
            for bi, (t0, nt) in enumerate(blks):
                emit_norm_block(nidx, bi, t0, nt, final)

        pending = []

        def pop_pending(n):
            for _ in range(n):
                if pending:
                    pending.pop(0)[1]()

        def flush_pending():
            while pending:
                pending.pop(0)[1]()

        def need_xn(bi):
            if any(b == bi for b, _ in pending):
                flush_pending()

        def norm_p1(bi, t0, nt):
            hres = [rh(j, bi) for j in range(8)]
            S.op("act", lambda h: h.activation(out=sq[:, :, 0:nt], in_=hT[:, :, t0:t0 + nt], func=AF.Square),
                 reads=hres, writes=[R["sq"]])

        def norm_p2(bi, t0, nt):
            fns = []
            for j in range(8):
                fns.append((lambda j: lambda h: h.matmul(
                    PB[6][:, 0:nt], ones_bf[:], sq[:, j, 0:nt], start=(j == 0), stop=(j == 7)))(j))
            S.group("pe", fns, reads=[R["sq"], R_ones], writes=[R_PB[6]])
            S.op("act", lambda h: h.activation(
                out=stdb[:, 0:nt], in_=PB[6][:, 0:nt], func=AF.Sqrt, bias=EPS_AP[:], scale=1.0 / D),
                reads=[R_PB[6], R_eps], writes=[R["stdb"]])
            S.op("dve", lambda h: h.reciprocal(out=rstd[:, 0:nt], in_=stdb[:, 0:nt]),
                 reads=[R["stdb"]], writes=[R["rstd"]])

        def norm_stt(nidx, j, bi, t0, nt, final):
            if final:
                S.op("dve", lambda h: h.scalar_tensor_tensor(
                    out=hT[:, j, t0:t0 + nt], in0=hT[:, j, t0:t0 + nt], scalar=gain(nidx, j),
                    in1=rstd[:, 0:nt], op0=ALU.mult, op1=ALU.mult),
                    reads=[R["rstd"], R_cst, rh(j, bi)], writes=[rh(j, bi)])
            else:
                S.op("dve", lambda h: h.scalar_tensor_tensor(
                    out=xnT[:, j, t0:t0 + nt], in0=hT[:, j, t0:t0 + nt], scalar=gain(nidx, j),
                    in1=rstd[:, 0:nt], op0=ALU.mult, op1=ALU.mult),
                    reads=[R["rstd"], R_cst, rh(j, bi)], writes=[R_xnT.setdefault(bi, Res())])

        def emit_norm_block(nidx, bi, t0, nt, final=False):
            norm_p1(bi, t0, nt)
            norm_p2(bi, t0, nt)
            for j in range(8):
                norm_stt(nidx, j, bi, t0, nt, final)

        def norm_cb(nidx, final):
            def cb(bi, t0, nt):
                norm_p1(bi, t0, nt)
                pending.append((bi, lambda: norm_p2(bi, t0, nt)))
                for j in range(8):
                    pending.append((bi, (lambda j: lambda: norm_stt(nidx, j, bi, t0, nt, final))(j)))
            return cb

        def emit_ffn(l, which, blks, T, old_arena, prenormed=False, after_block=None):
            cg = l * 3 + (0 if which == 1 else 2)
            Wg = wb[(l, f"ffn{which}_gate")].rearrange("(k p) n -> p k n", p=128)
            Wu = wb[(l, f"ffn{which}_up")].rearrange("(k p) n -> p k n", p=128)
            Wd = wb[(l, f"ffn{which}_down")].rearrange("(f p) n -> p f n", p=128)
            hid = arena[:, 0:NFF * T].rearrange("p (f t) -> p f t", f=NFF)
            R_hid = {}
            for f in range(NFF):
                for bi in range(len(blks)):
                    R_hid[(f, bi)] = Res()
            fence(list(R_hid.values()), old_arena)
            if not prenormed:
                emit_norm(l * 3 + (0 if which == 1 else 2), blks)
            pbi = 0
            f0 = 0
            while f0 < NFF:
                nf = min(4, NFF - f0)
                nco = nf * 128
                wg, rg = load_tile(Wg[:, :, f0 * 128:f0 * 128 + nco], v_k(nco), cg)
                wu, ru = load_tile(Wu[:, :, f0 * 128:f0 * 128 + nco], v_k(nco), cg)
                for bi, (t0, nt) in enumerate(blks):
                    need_xn(bi)
                    for fi in range(nf):
                        pop_pending(3)
                        f = f0 + fi
                        pg = pbi % 2
                        pu = 2 + pbi % 2
                        pbi += 1
                        for (w_, r_, pp) in ((wg, rg, pg), (wu, ru, pu)):
                            fns = [(lambda w_, k, fi, t0, nt, pp: lambda h: h.matmul(
                                PB[pp][:, 0:nt], w_[:, k, fi * 128:(fi + 1) * 128], xnT[:, k, t0:t0 + nt],
                                start=(k == 0), stop=(k == 7)))(w_, k, fi, t0, nt, pp) for k in range(8)]
                            S.group("pe", fns, reads=[r_, R_xnT[bi]], writes=[R_PB[pp]])
                        sgi = pg
                        S.op("act", (lambda pg, nt, sgi: lambda h: h.activation(
                            out=sgb[sgi][:, 0:nt], in_=PB[pg][:, 0:nt], func=AF.Silu))(pg, nt, sgi),
                            reads=[R_PB[pg]], writes=[R[f"sgb{sgi}"]])
                        S.op("dve", (lambda f, t0, nt, sgi, pu: lambda h: h.tensor_tensor(
                            out=hid[:, f, t0:t0 + nt], in0=sgb[sgi][:, 0:nt], in1=PB[pu][:, 0:nt],
                            op=ALU.mult))(f, t0, nt, sgi, pu),
                            reads=[R[f"sgb{sgi}"], R_PB[pu]], writes=[R_hid[(f, bi)]])
                f0 += nf
            for bi, (t0, nt) in enumerate(blks):
                for j in range(8):
                    wd, rd = load_tile(Wd[:, :, j * 128:(j + 1) * 128], v_f(NFF, 128), cg)
                    po = 4 + (bi * 8 + j) % 2
                    fns = [(lambda f, t0, nt, po, wd: lambda h: h.matmul(
                        PB[po][:, 0:nt], wd[:, f, :], hid[:, f, t0:t0 + nt],
                        start=(f == 0), stop=(f == NFF - 1)))(f, t0, nt, po, wd) for f in range(NFF)]
                    S.group("pe", fns, reads=[rd] + [R_hid[(f, bi)] for f in range(NFF)], writes=[R_PB[po]])
                    S.op("dve", (lambda j, t0, nt, po: lambda h: h.scalar_tensor_tensor(
                        out=hT[:, j, t0:t0 + nt], in0=PB[po][:, 0:nt], scalar=0.5, in1=hT[:, j, t0:t0 + nt],
                        op0=ALU.mult, op1=ALU.add))(j, t0, nt, po),
                        reads=[R_PB[po], rh(j, bi)], writes=[rh(j, bi)])
                    if j >= 1:
                        pop_pending(2)
                if after_block is not None:
                    after_block(bi, t0, nt)
            return list(R_hid.values())

        def emit_mixer(l, blks, T, c0, nck, old_arena, prenormed=False, after_block=None):
            cg = l * 3 + 1
            Win = wb[(l, "w_in")].rearrange("(k p) n -> p k n", p=128)
            Wru = wb[(l, "w_ret_up")].rearrange("(k p) n -> p k n", p=128)
            Wpu = wb[(l, "w_pool_up")].rearrange("(k p) n -> p k n", p=128)
            Wo = wb[(l, "w_out")].rearrange("(k p) n -> p k n", p=128)
            retinT = arena[:, 0:4 * T].rearrange("p (h t) -> p h t", h=4)
            pmT = arena[:, 4 * T:8 * T].rearrange("p (g t) -> p g t", g=4)
            mixT = arena[:, 8 * T:16 * T].rearrange("p (j t) -> p j t", j=8)
            R_ret = [Res() for _ in blks]
            R_pm = [Res() for _ in blks]
            R_mix = {}
            for j in range(8):
                for bi in range(len(blks)):
                    R_mix[(j, bi)] = Res()
            fence(R_ret + R_pm + list(R_mix.values()), old_arena)

            if c0 == 0:
                S.wait_tok("sp", conv_tok[cg])
                S.dma("sp", f"maps{l}", maps_sem[l],
                      lambda h: h.dma_start(
                          out=maps_sb[l][:], in_=wb[(l, "pool_maps")].rearrange("(g p) n -> p g n", p=128)),
                      writes=[R_maps[l]])
            flush_pending()
            if not prenormed:
                emit_norm(l * 3 + 1, blks)
            wt = [load_tile(Win[:, :, i * 512:(i + 1) * 512], v_k(512), cg) for i in range(4)]
            def m1_A_pe(cl):
                c = c0 + cl
                tcol = cl * 128
                bi = tcol // 512
                ri = c % 2
                S.dma("pool", f"rot{ri}", rot_sem[ri],
                      (lambda ri, c: lambda h: h.dma_start(out=rot_sb[ri][:], in_=rot[c]))(ri, c),
                      writes=[R_rot[ri]])
                for i in range(4):
                    w_, r_ = wt[i]
                    fns = [(lambda w_, k, i, tcol: lambda h: h.matmul(
                        PB[i][:, :], xnT[:, k, tcol:tcol + 128], w_[:, k, :],
                        start=(k == 0), stop=(k == 7)))(w_, k, i, tcol) for k in range(8)]
                    S.group("pe", fns, reads=[r_, R_xnT[bi]], writes=[R_PB[i]])

            def rotary(eng, srcf, rsrc, cs, sn, dst, rdst, ta, rta_, tb, rtb_, ri):
                x1 = srcf[:, :, 0:64]
                x2 = srcf[:, :, 64:128]
                S.op(eng, (lambda x1, cs, ta: lambda h: h.tensor_tensor(out=ta[:], in0=x1, in1=cs, op=ALU.mult))(x1, cs, ta),
                     reads=[rsrc, R_rot[ri]], writes=[rta_])
                S.op(eng, (lambda x2, sn, tb: lambda h: h.tensor_tensor(out=tb[:], in0=x2, in1=sn, op=ALU.mult))(x2, sn, tb),
                     reads=[rsrc, R_rot[ri]], writes=[rtb_])
                S.op(eng, (lambda dst, ta, tb: lambda h: h.tensor_tensor(out=dst[:, :, 0:64], in0=ta[:], in1=tb[:], op=ALU.subtract))(dst, ta, tb),
                     reads=[rta_, rtb_], writes=[rdst])
                S.op(eng, (lambda x1, sn, ta: lambda h: h.tensor_tensor(out=ta[:], in0=x1, in1=sn, op=ALU.mult))(x1, sn, ta),
                     reads=[rsrc, R_rot[ri]], writes=[rta_])
                S.op(eng, (lambda x2, cs, tb: lambda h: h.tensor_tensor(out=tb[:], in0=x2, in1=cs, op=ALU.mult))(x2, cs, tb),
                     reads=[rsrc, R_rot[ri]], writes=[rtb_])
                S.op(eng, (lambda dst, ta, tb: lambda h: h.tensor_tensor(out=dst[:, :, 64:128], in0=ta[:], in1=tb[:], op=ALU.add))(dst, ta, tb),
                     reads=[rta_, rtb_], writes=[rdst])

            def m1_A_rest(cl):
                c = c0 + cl
                par = cl % 2
                ri = c % 2
                S.op("act", (lambda par: lambda h: h.copy(out=qf[par][:], in_=PB[0][:, :].rearrange("p (h e) -> p h e", h=4)))(par),
                     reads=[R_PB[0]], writes=[R_qf[par]])
                S.op("act", (lambda par: lambda h: h.copy(out=kf[par][:], in_=PB[1][:, :].rearrange("p (h e) -> p h e", h=4)))(par),
                     reads=[R_PB[1]], writes=[R_kf[par]])
                S.op("act", (lambda par: lambda h: h.copy(out=vtokd[par][:], in_=PB[2][:, :]))(par),
                     reads=[R_PB[2]], writes=[R_vt[par]])
                S.op("act", (lambda par: lambda h: h.activation(out=sgrd[par][:], in_=PB[3][:, :], func=AF.Silu))(par),
                     reads=[R_PB[3]], writes=[R_sg[par]])
                rt = rot_sb[ri][:].rearrange("p (a h e) -> p a h e", a=4, h=4)
                rotary("pool", qf[par], R_qf[par], rt[:, 0], rt[:, 1], qtd[par], R_qt[par], pta, R["pta"], ptb, R["ptb"], ri)
                rotary("dve", kf[par], R_kf[par], rt[:, 2], rt[:, 3], ktd[par], R_kt[par], rta, R["rta"], rtb, R["rtb"], ri)

            def m1_B(cl):
                par = cl % 2
                qt_, kt_, vt_ = qtd[par], ktd[par], vtokd[par]
                fns = []
                for hh in range(4):
                    fns.append((lambda hh, qt_: lambda h: h.transpose(PT[:, hh, :], qt_[:, hh, :], ident_bf[:]))(hh, qt_))
                for hh in range(4):
                    fns.append((lambda hh, kt_: lambda h: h.transpose(PT[:, 4 + hh, :], kt_[:, hh, :], ident_bf[:]))(hh, kt_))
                S.group("pe", fns, reads=[R_qt[par], R_kt[par], R_identbf], writes=[R_PT])
                S.op("act", lambda h: h.copy(out=qkT[:], in_=PT[:]), reads=[R_PT], writes=[R["qkT"]])
                fns = [(lambda hh: lambda h: h.matmul(
                    PB[4][:, hh * 128:(hh + 1) * 128], qkT[:, 4 + hh, :], qkT[:, hh, :], start=True, stop=True))(hh)
                    for hh in range(4)]
                S.group("pe", fns, reads=[R["qkT"]], writes=[R_PB[4]])
                S.op("dve", lambda h: h.tensor_tensor(
                    out=sT[:], in0=PB[4][:, :].rearrange("p (h c) -> p h c", h=4),
                    in1=maskT.unsqueeze(1).broadcast_to([128, 4, 128]), op=ALU.mult),
                    reads=[R_PB[4], R_cst], writes=[R["sT"]])
                fns = []
                for hh in range(4):
                    fns.append((lambda hh, vt_: lambda h: h.matmul(
                        PB[5][:, hh * 128:(hh + 1) * 128], sT[:, hh, :], vt_[:, hh * 128:(hh + 1) * 128],
                        start=True, stop=False))(hh, vt_))
                    fns.append((lambda hh: lambda h: h.matmul(
                        PB[5][:, hh * 128:(hh + 1) * 128], qkT[:, hh, :], S16[l][:, hh, :],
                        start=False, stop=True))(hh))
                S.group("pe", fns, reads=[R["sT"], R_vt[par], R["qkT"], R_S16[l]], writes=[R_PB[5]])
                fns = [(lambda hh, kt_, vt_: lambda h: h.matmul(
                    PB[6][:, hh * 128:(hh + 1) * 128], kt_[:, hh, :], vt_[:, hh * 128:(hh + 1) * 128],
                    start=True, stop=True))(hh, kt_, vt_) for hh in range(4)]
                S.group("pe", fns, reads=[R_kt[par], R_vt[par]], writes=[R_PB[6]])
                S.op("dve", lambda h: h.tensor_tensor(
                    out=stmp[:], in0=PB[6][:, :].rearrange("p (h e) -> p h e", h=4), in1=S32[l][:], op=ALU.add),
                    reads=[R_PB[6], R_S32[l]], writes=[R["stmp"]])
                S.op("dve", lambda h: h.tensor_tensor(out=S32[l][:], in0=stmp[:], in1=CDt, op=ALU.mult),
                     reads=[R["stmp"], R_cst], writes=[R_S32[l]])
                S.op("act", lambda h: h.copy(out=S16[l][:], in_=S32[l][:]),
                     reads=[R_S32[l]], writes=[R_S16[l]])

            def m1_C(cl):
                par = cl % 2
                P5v = PB[5][:, :].rearrange("p (h e) -> p h e", h=4)
                S.op("dve", lambda h: h.tensor_reduce(out=gst[:, 0:4], in_=P5v, axis=AX.X, op=ALU.add),
                     reads=[R_PB[5]], writes=[R["gst"]])
                S.op("dve", lambda h: h.tensor_scalar(out=gst[:, 4:8], in0=gst[:, 0:4], scalar1=-1.0 / 128,
                                                      scalar2=None, op0=ALU.mult),
                     reads=[R["gst"]], writes=[R["gst"]])
                S.op("dve", lambda h: h.tensor_tensor(
                    out=gt[:], in0=P5v, in1=gst[:, 4:8].unsqueeze(2).broadcast_to([128, 4, 128]), op=ALU.add),
                    reads=[R_PB[5], R["gst"]], writes=[R["gt"]])
                S.op("pool", lambda h: h.tensor_tensor(out=gsq[:], in0=gt[:], in1=gt[:], op=ALU.mult),
                     reads=[R["gt"]], writes=[R["gsq"]])
                S.op("dve", lambda h: h.tensor_reduce(out=gst[:, 8:12], in_=gsq[:], axis=AX.X, op=ALU.add),
                     reads=[R["gsq"], R["gst2"]], writes=[R["gst"]])
                S.op("act", lambda h: h.activation(out=gst[:, 8:12], in_=gst[:, 8:12], func=AF.Sqrt,
                                                   bias=EPS_AP[:], scale=1.0 / 128),
                     reads=[R["gst"], R_eps], writes=[R["gst"]])
                S.op("dve", lambda h: h.reciprocal(out=gst2[:, 0:4], in_=gst[:, 8:12]),
                     reads=[R["gst"]], writes=[R["gst2"]])
                S.op("pool", lambda h: h.tensor_tensor(
                    out=gsq[:], in0=gt[:], in1=gst2[:, 0:4].unsqueeze(2).broadcast_to([128, 4, 128]), op=ALU.mult),
                    reads=[R["gt"], R["gst2"]], writes=[R["gsq"]])
                S.op("pool", (lambda par: lambda h: h.tensor_tensor(
                    out=gy2[:], in0=gsq[:], in1=sgrd[par][:].rearrange("p (h e) -> p h e", h=4), op=ALU.mult))(par),
                    reads=[R["gsq"], R_sg[par]], writes=[R["gy2"]])

            def m1_D(cl):
                tcol = cl * 128
                bi = tcol // 512
                fns = [(lambda hh: lambda h: h.transpose(PT[:, hh, :], gy2[:, hh, :], ident_bf[:]))(hh)
                       for hh in range(4)]
                S.group("pe", fns, reads=[R["gy2"], R_identbf], writes=[R_PT])
                S.op("act", (lambda tcol: lambda h: h.copy(out=retinT[:, :, tcol:tcol + 128], in_=PT[:, 0:4, :]))(tcol),
                     reads=[R_PT], writes=[R_ret[bi]])

            m1_A_pe(0)
            m1_A_rest(0)
            for cl in range(nck):
                m1_B(cl)
                if cl + 1 < nck:
                    m1_A_pe(cl + 1)
                m1_C(cl)
                if cl + 1 < nck:
                    m1_A_rest(cl + 1)
                m1_D(cl)
            wu_, ru_ = load_tile(Win[:, :, 4 * 512:5 * 512], v_k(512), cg)
            pbi = 0
            for bi, (t0, nt) in enumerate(blks):
                E = 16 + nt
                for g in range(4):
                    pp = pbi % 2
                    pm = 2 + pbi % 2
                    pbi += 1
                    fns = [(lambda k, g, t0, nt, pp: lambda h: h.matmul(
                        PB[pp][:, 0:nt], wu_[:, k, g * 128:(g + 1) * 128], xnT[:, k, t0:t0 + nt],
                        start=(k == 0), stop=(k == 7)))(k, g, t0, nt, pp) for k in range(8)]
                    S.group("pe", fns, reads=[ru_, R_xnT[bi]], writes=[R_PB[pp]])
                    S.op("dve", (lambda l, g: lambda h: h.tensor_copy(out=ub[:, 0:16], in_=halo[l][:, g, :]))(l, g),
                         reads=[R_halo[l]], writes=[R["ub"]])
                    S.op("act", (lambda nt, pp: lambda h: h.copy(out=ub[:, 16:16 + nt], in_=PB[pp][:, 0:nt]))(nt, pp),
                         reads=[R_PB[pp]], writes=[R["ub"]])
                    S.op("dve", (lambda l, g, nt: lambda h: h.tensor_copy(out=halo[l][:, g, :], in_=ub[:, nt:nt + 16]))(l, g, nt),
                         reads=[R["ub"]], writes=[R_halo[l]])
                    src, rsrc = ub, "ub"
                    tmps = [(pa, "pa"), (pb, "pb")]
                    for kk in range(1, g + 2):
                        sh = 2 ** (kk - 1)
                        d0 = 2 ** kk - 1
                        dst, rdst = tmps[(kk - 1) % 2]
                        S.op("dve", (lambda src, dst, sh, d0, E: lambda h: h.tensor_tensor(
                            out=dst[:, d0:E], in0=src[:, d0:E], in1=src[:, d0 - sh:E - sh], op=ALU.add))(src, dst, sh, d0, E),
                            reads=[R[rsrc]], writes=[R[rdst]])
                        src, rsrc = dst, rdst
                    w = 2 ** (g + 1)
                    S.op("dve", (lambda src, nt, w: lambda h: h.scalar_tensor_tensor(
                        out=pooled[:, 0:nt], in0=src[:, 16:16 + nt], scalar=1.0 / w, in1=ub[:, 16:16 + nt],
                        op0=ALU.mult, op1=ALU.subtract))(src, nt, w),
                        reads=[R[rsrc], R["ub"]], writes=[R["pooled"]])
                    if c0 == 0 and bi == 0:
                        S.op("dve", (lambda src, g: lambda h: h.tensor_tensor(
                            out=p16[:], in0=src[:, 16 + 112:16 + 128], in1=Rtab[:, g, :], op=ALU.mult))(src, g),
                            reads=[R[rsrc], R_cst], writes=[R["p16"]])
                        S.op("dve", lambda h: h.tensor_tensor(
                            out=pooled[:, 112:128], in0=p16[:], in1=ub[:, 16 + 112:16 + 128], op=ALU.subtract),
                            reads=[R["p16"], R["ub"], R["pooled"]], writes=[R["pooled"]])
                    S.op("pe", (lambda l, g, nt, pm: lambda h: h.matmul(
                        PB[pm][:, 0:nt], maps_sb[l][:, g, :], pooled[:, 0:nt], start=True, stop=True))(l, g, nt, pm),
                        reads=[R_maps[l], R["pooled"]], writes=[R_PB[pm]])
                    S.op("act", (lambda l, g, t0, nt, pm: lambda h: h.activation(
                        out=pmT[:, g, t0:t0 + nt], in_=PB[pm][:, 0:nt], func=AF.Copy, scale=pscale(l, g)))(l, g, t0, nt, pm),
                        reads=[R_PB[pm], R_cst], writes=[R_pm[bi]])
            wru, rru = load_tile(Wru, v_f(4, 1024), cg)
            wpu, rpu = load_tile(Wpu, v_f(4, 1024), cg)
            it = 0
            for jg in range(2):
                wga, rga = load_tile(Win[:, :, (5 + jg) * 512:(6 + jg) * 512], v_k(512), cg)
                wgb, rgb = load_tile(Win[:, :, (7 + jg) * 512:(8 + jg) * 512], v_k(512), cg)
                for jj in range(4):
                    j = jg * 4 + jj
                    for bi, (t0, nt) in enumerate(blks):
                        fns = [(lambda k, jj, t0, nt, wga: lambda h: h.matmul(
                            PB[0][:, 0:nt], wga[:, k, jj * 128:(jj + 1) * 128], xnT[:, k, t0:t0 + nt],
                            start=(k == 0), stop=(k == 7)))(k, jj, t0, nt, wga) for k in range(8)]
                        S.group("pe", fns, reads=[rga, R_xnT[bi]], writes=[R_PB[0]])
                        fns = [(lambda k, jj, t0, nt, wgb: lambda h: h.matmul(
                            PB[1][:, 0:nt], wgb[:, k, jj * 128:(jj + 1) * 128], xnT[:, k, t0:t0 + nt],
                            start=(k == 0), stop=(k == 7)))(k, jj, t0, nt, wgb) for k in range(8)]
                        S.group("pe", fns, reads=[rgb, R_xnT[bi]], writes=[R_PB[1]])
                        fns = [(lambda hh, j, t0, nt: lambda h: h.matmul(
                            PB[2][:, 0:nt], wru[:, hh, j * 128:(j + 1) * 128], retinT[:, hh, t0:t0 + nt],
                            start=(hh == 0), stop=(hh == 3)))(hh, j, t0, nt) for hh in range(4)]
                        S.group("pe", fns, reads=[rru, R_ret[bi]], writes=[R_PB[2]])
                        fns = [(lambda g, j, t0, nt: lambda h: h.matmul(
                            PB[3][:, 0:nt], wpu[:, g, j * 128:(j + 1) * 128], pmT[:, g, t0:t0 + nt],
                            start=(g == 0), stop=(g == 3)))(g, j, t0, nt) for g in range(4)]
                        S.group("pe", fns, reads=[rpu, R_pm[bi]], writes=[R_PB[3]])
                        a = it % 2
                        it += 1
                        S.op("act", (lambda nt, a: lambda h: h.activation(
                            out=sgb[a][:, 0:nt], in_=PB[0][:, 0:nt], func=AF.Sigmoid))(nt, a),
                            reads=[R_PB[0]], writes=[R[f"sgb{a}"]])
                        S.op("act", (lambda nt, a: lambda h: h.activation(
                            out=sgc[a][:, 0:nt], in_=PB[1][:, 0:nt], func=AF.Sigmoid))(nt, a),
                            reads=[R_PB[1]], writes=[R[f"sgc{a}"]])
                        S.op("dve", (lambda nt, a: lambda h: h.tensor_tensor(
                            out=sgb[a][:, 0:nt], in0=sgb[a][:, 0:nt], in1=PB[2][:, 0:nt], op=ALU.mult))(nt, a),
                            reads=[R[f"sgb{a}"], R_PB[2]], writes=[R[f"sgb{a}"]])
                        S.op("dve", (lambda nt, a: lambda h: h.tensor_tensor(
                            out=sgc[a][:, 0:nt], in0=sgc[a][:, 0:nt], in1=PB[3][:, 0:nt], op=ALU.mult))(nt, a),
                            reads=[R[f"sgc{a}"], R_PB[3]], writes=[R[f"sgc{a}"]])
                        S.op("dve", (lambda j, t0, nt, a: lambda h: h.tensor_tensor(
                            out=mixT[:, j, t0:t0 + nt], in0=sgb[a][:, 0:nt], in1=sgc[a][:, 0:nt], op=ALU.add))(j, t0, nt, a),
                            reads=[R[f"sgb{a}"], R[f"sgc{a}"]], writes=[R_mix[(j, bi)]])
            wos = [load_tile(Wo[:, :, jh * 512:(jh + 1) * 512], v_k(512), cg) for jh in range(2)]
            for bi, (t0, nt) in enumerate(blks):
                for jh in range(2):
                    wo, ro = wos[jh]
                    for jj in range(4):
                        j2 = jh * 4 + jj
                        po = 4 + (bi * 8 + j2) % 2
                        fns = [(lambda k, jj, t0, nt, po, wo: lambda h: h.matmul(
                            PB[po][:, 0:nt], wo[:, k, jj * 128:(jj + 1) * 128], mixT[:, k, t0:t0 + nt],
                            start=(k == 0), stop=(k == 7)))(k, jj, t0, nt, po, wo) for k in range(8)]
                        S.group("pe", fns, reads=[ro] + [R_mix[(k, bi)] for k in range(8)], writes=[R_PB[po]])
                        S.op("dve", (lambda j2, t0, nt, po: lambda h: h.tensor_tensor(
                            out=hT[:, j2, t0:t0 + nt], in0=PB[po][:, 0:nt], in1=hT[:, j2, t0:t0 + nt], op=ALU.add))(j2, t0, nt, po),
                            reads=[R_PB[po], rh(j2, bi)], writes=[rh(j2, bi)])
                        if j2 >= 1:
                            pop_pending(2)
                if after_block is not None:
                    after_block(bi, t0, nt)
            return R_ret + R_pm + list(R_mix.values())


        out_n = 0
        old_arena = []
        for (c0, c1) in passes:
            nck = c1 - c0
            T = nck * 128
            blks = blocks_of(T)
            for cl in range(nck):
                c = c0 + cl
                xi = c % 2
                if c == 0:
                    S.op("dve", (lambda xi: lambda h: h.memset(xs[xi][:], 0.0))(xi), writes=[R_xs[xi]])
                    S.dma("sp", f"xs{xi}", xs_sem[xi],
                          (lambda xi: lambda h: h.dma_start(out=xs[xi][112:128, :], in_=meta))(xi),
                          reads=[R_xs[xi]], writes=[R_xs[xi]])
                else:
                    S.dma("sp", f"xs{xi}", xs_sem[xi],
                          (lambda xi, c: lambda h: h.dma_start(out=xs[xi][:], in_=x[(c - 1) * 128:c * 128, :]))(xi, c),
                          writes=[R_xs[xi]])
                tcol = cl * 128
                bi = tcol // 512
                for half in range(2):
                    pbk = half
                    fns = [(lambda jj, half, xi, pbk: lambda h: h.transpose(
                        PB[pbk][:, jj * 128:(jj + 1) * 128], xs[xi][:, (half * 4 + jj) * 128:(half * 4 + jj + 1) * 128],
                        identf))(jj, half, xi, pbk) for jj in range(4)]
                    S.group("pe", fns, reads=[R_xs[xi], R_cst], writes=[R_PB[pbk]])
                    S.op("act", (lambda half, tcol, pbk: lambda h: h.copy(
                        out=hT[:, half * 4:half * 4 + 4, tcol:tcol + 128],
                        in_=PB[pbk][:, :].rearrange("p (j t) -> p j t", j=4)))(half, tcol, pbk),
                        reads=[R_PB[pbk]], writes=[rh(j, bi) for j in range(half * 4, half * 4 + 4)])
            stages = []
            for l in range(layers):
                if do_ffn:
                    stages.append(("ffn", l, 1, l * 3 + 0))
                if do_mix:
                    stages.append(("mix", l, 0, l * 3 + 1))
                if do_ffn:
                    stages.append(("ffn", l, 2, l * 3 + 2))
            for si, (kind, l, which, nidx) in enumerate(stages):
                if si + 1 < len(stages):
                    nxt = stages[si + 1][3]
                    cb = norm_cb(nxt, False)
                else:
                    cb = norm_cb(6, True)
                if c0 == 0 and si + 2 < 3 * DEPTH and staged_conv:
                    emit_conv(si + 2, after=[R_xnT[0]] if 0 in R_xnT else [])
                if kind == "ffn":
                    old_arena = emit_ffn(l, which, blks, T, old_arena, prenormed=(si > 0), after_block=cb)
                else:
                    old_arena = emit_mixer(l, blks, T, c0, nck, old_arena, prenormed=(si > 0), after_block=cb)
            flush_pending()
            for cl in range(nck):
                c = c0 + cl
                if c == 0:
                    continue
                tcol = cl * 128
                bi = tcol // 512
                xi = out_n % 2
                out_n += 1
                for half in range(2):
                    pbk = 2 + half
                    fns = [(lambda jj, half, tcol, pbk: lambda h: h.transpose(
                        PB[pbk][:, jj * 128:(jj + 1) * 128], hT[:, half * 4 + jj, tcol:tcol + 128], identf))(jj, half, tcol, pbk)
                        for jj in range(4)]
                    S.group("pe", fns, reads=[rh(j, bi) for j in range(half * 4, half * 4 + 4)] + [R_cst],
                            writes=[R_PB[pbk]])
                    S.op("act", (lambda half, xi, pbk: lambda h: h.copy(
                        out=xs[xi][:, half * 512:(half + 1) * 512], in_=PB[pbk][:, :]))(half, xi, pbk),
                        reads=[R_PB[pbk]], writes=[R_xs[xi]])
                S.dma("pool", f"xs{xi}", xs_sem[xi],
                      (lambda xi, c: lambda h: h.dma_start(out=y[(c - 1) * 128:c * 128, :], in_=xs[xi][:]))(xi, c),
                      reads=[R_xs[xi]])
        for xi in range(2):
            if R_xs[xi].r:
                for tok in R_xs[xi].r.values():
                    S.wait_tok("pool", tok)

        with nc.Block() as block:
            @block.tensor
            def _(h):
                S.replay("pe", h)

            @block.scalar
            def _(h):
                S.replay("act", h)

            @block.vector
            def _(h):
                S.replay("dve", h)

            @block.gpsimd
            def _(h):
                S.replay("pool", h)

            @block.sync
            def _(h):
                S.replay("sp", h)
    return nc


def make_consts(nch_total, inputs):
    cst = np.zeros((128, NCST), np.float32)
    cst[:, C_ID:C_ID + 128] = np.eye(128, dtype=np.float32)
    m = np.arange(128)[:, None]
    c = np.arange(128)[None, :]
    cst[:, C_MASK:C_MASK + 128] = (c >= m).astype(np.float32)
    gam = 1.0 - 2.0 ** (-5.0 - np.arange(4, dtype=np.float64))
    for h in range(4):
        cst[:, C_CD + h * 128:C_CD + (h + 1) * 128] = np.float32(gam[h] ** 128)
    for w_i, w in enumerate((2, 4, 8, 16)):
        t = np.arange(16)
        cst[:, C_R + w_i * 16:C_R + (w_i + 1) * 16] = (1.0 / np.minimum(t + 1.0, float(w))).astype(np.float32)[None, :]
    half = 64
    inv_freq = (np.float32(10000.0) ** (-np.arange(half, dtype=np.float32) / half)).astype(np.float32)
    rot = np.zeros((nch_total, 128, 4, 4, 64), np.float32)
    i = np.arange(128, dtype=np.float64)
    for ch in range(nch_total):
        pos = (np.arange(128, dtype=np.float32) + np.float32(128 * ch - 112)).astype(np.float32)
        ang = (pos[:, None] * inv_freq[None, :]).astype(np.float32)
        cs = np.cos(ang).astype(np.float64)
        sn = np.sin(ang).astype(np.float64)
        for h in range(4):
            qs = (128.0 ** -0.5) * gam[h] ** (i + 1.0)
            ks = gam[h] ** (-(i + 1.0))
            rot[ch, :, 0, h, :] = cs * qs[:, None]
            rot[ch, :, 1, h, :] = sn * qs[:, None]
            rot[ch, :, 2, h, :] = cs * ks[:, None]
            rot[ch, :, 3, h, :] = sn * ks[:, None]
    return cst, rot.reshape(nch_total, 128, 1024)


def fill_param_consts(cst, inputs):
    norms = [inputs["ffn1_norm"][0], inputs["mix_norm"][0], inputs["ffn2_norm"][0],
             inputs["ffn1_norm"][1], inputs["mix_norm"][1], inputs["ffn2_norm"][1], inputs["final_norm"]]
    for n, g in enumerate(norms):
        cst[:, C_GAIN + n * 8:C_GAIN + (n + 1) * 8] = np.asarray(g, np.float32).reshape(8, 128).T
    for l in range(DEPTH):
        cst[:, C_PSC + l * 4:C_PSC + (l + 1) * 4] = np.asarray(inputs["pool_scale"][l], np.float32).reshape(4, 128).T
    return cst


PASSES_FULL = [(0, 3), (3, 9), (9, 15), (15, 21), (21, 27), (27, 33)]


def run(inputs, passes, xrows, n_cores, **kw):
    inputs = {k: np.asarray(v) for k, v in inputs.items()}
    nch_total = passes[-1][1]
    cst, rot = make_consts(nch_total, inputs)
    cst = fill_param_consts(cst, inputs)
    nc = build_program(passes, xrows, **kw)
    base = {"meta": np.ascontiguousarray(inputs["meta"], np.float32), "cst": cst, "rot": rot}
    for l in range(DEPTH):
        for n in WNAMES:
            a = np.asarray(inputs[n][l], np.float32)
            base[f"{n}_{l}"] = np.ascontiguousarray(a.reshape(WSHAPES[n]))
    in_maps = []
    for b in range(n_cores):
        m = dict(base)
        m["x"] = np.ascontiguousarray(inputs["x"][b, :xrows], np.float32)
        in_maps.append(m)
    res = run_bass_kernel_spmd(nc, in_maps, core_ids=list(range(n_cores)))
    return np.stack([r["y"] for r in res.results], axis=0)


def kernel(**inputs):
    return run(inputs, PASSES_FULL, SEQ, 8).astype(np.float32)
```
